# Optimizing a Trainium2 kernel written in Bass

```python
import math
import jax
import jax.numpy as jnp
from jax import lax
import numpy as np

D_MODEL = 1024
BATCH = 4
SEQ = 4096
DEPTH = 4

N_META = 16
CHUNK = 64
META_PAD = CHUNK - N_META
CONV_K = 4
N_BRANCH = 3
BRANCH_WIDTH = 768

S5_WIDTH = BRANCH_WIDTH
S5_GROUP = 16
S5_GROUPS = S5_WIDTH // S5_GROUP
S5_STATE = 64
S5_STEP_MIN = 1e-3
S5_STEP_MAX = 1e-1

SSD_HEAD_DIM = 64
SSD_HEADS = BRANCH_WIDTH // SSD_HEAD_DIM
SSD_WIDTH = SSD_HEADS * SSD_HEAD_DIM
SSD_GROUPS = 2
SSD_STATE = 128
SSD_CONV_WIDTH = SSD_WIDTH + 2 * SSD_GROUPS * SSD_STATE

GDN_HEAD_DIM = 128
GDN_HEADS = BRANCH_WIDTH // GDN_HEAD_DIM
GDN_WIDTH = GDN_HEADS * GDN_HEAD_DIM

IN_SPLITS = (S5_WIDTH, S5_WIDTH,
             SSD_CONV_WIDTH, SSD_HEADS, SSD_WIDTH,
             3 * GDN_WIDTH, GDN_HEADS, GDN_HEADS, GDN_WIDTH,
             N_BRANCH * D_MODEL)
IN_WIDTH = sum(IN_SPLITS)

ALPHA = (2 * DEPTH) ** 0.25
BETA = (8 * DEPTH) ** -0.25
LN_EPS = 1e-5

kernel_name = 'hybrid_s5_ssd_gdn_gated_merge'


def _split_points():
    pts, acc = [], 0
    for w in IN_SPLITS[:-1]:
        acc += w
        pts.append(acc)
    return pts


def _layer_norm(z, g, b):
    zf = z.astype(jnp.float32)
    mu = jnp.mean(zf, axis=-1, keepdims=True)
    var = jnp.mean(jnp.square(zf - mu), axis=-1, keepdims=True)
    return ((zf - mu) * lax.rsqrt(var + LN_EPS) * g + b).astype(z.dtype)


def _rms_norm(z, g):
    zf = z.astype(jnp.float32)
    return zf * lax.rsqrt(jnp.mean(zf * zf, axis=-1, keepdims=True) + LN_EPS) * g.astype(jnp.float32)


def _l2norm(z):
    return z * lax.rsqrt(jnp.sum(z * z, axis=-1, keepdims=True) + 1e-6)


def _front_pad(z):
    return jnp.pad(z, [(0, 0), (META_PAD, 0)] + [(0, 0)] * (z.ndim - 2))


def _causal_dwconv(u, w):
    k, ch = w.shape
    return lax.conv_general_dilated(u, w.astype(u.dtype)[:, None, :], window_strides=(1,),
                                    padding=[(k - 1, 0)], dimension_numbers=('NWC', 'WIO', 'NWC'),
                                    feature_group_count=ch)


def _complex_affine_combine(earlier, later):
    a1r, a1i, b1r, b1i = earlier
    a2r, a2i, b2r, b2i = later
    return (a2r * a1r - a2i * a1i, a2r * a1i + a2i * a1r,
            a2r * b1r - a2i * b1i + b2r, a2r * b1i + a2i * b1r + b2i)


def _s5_branch(u, z, a_re, a_im, log_step, b_re, b_im, c_re, c_im, d, w_glu, b_glu):
    bsz, t, _ = u.shape
    uf = u.astype(jnp.float32)
    ug = uf.reshape(bsz, t, S5_GROUPS, S5_GROUP)
    lam_re = jnp.minimum(a_re.astype(jnp.float32), -1e-4)
    lam_im = a_im.astype(jnp.float32)
    step = jnp.exp(log_step.astype(jnp.float32))[:, None]
    mag = jnp.exp(lam_re * step)
    abar_re, abar_im = mag * jnp.cos(lam_im * step), mag * jnp.sin(lam_im * step)
    den = lam_re * lam_re + lam_im * lam_im
    nr, ni = abar_re - 1.0, abar_im
    coef_re = (nr * lam_re + ni * lam_im) / den
    coef_im = (ni * lam_re - nr * lam_im) / den
    bbar_re = coef_re[..., None] * b_re - coef_im[..., None] * b_im
    bbar_im = coef_re[..., None] * b_im + coef_im[..., None] * b_re
    bu_re = jnp.einsum('btgc,gpc->btgp', ug, bbar_re)
    bu_im = jnp.einsum('btgc,gpc->btgp', ug, bbar_im)
    ae_re = jnp.broadcast_to(abar_re, (1, t, S5_GROUPS, S5_STATE))
    ae_im = jnp.broadcast_to(abar_im, (1, t, S5_GROUPS, S5_STATE))
    _, _, s_re, s_im = lax.associative_scan(_complex_affine_combine, (ae_re, ae_im, bu_re, bu_im), axis=1)
    y = jnp.einsum('btgp,gcp->btgc', s_re, c_re) - jnp.einsum('btgp,gcp->btgc', s_im, c_im)
    y = y.reshape(bsz, t, S5_WIDTH) + d * uf
    v = jax.nn.gelu(y)
    v = v * jax.nn.sigmoid(v @ w_glu + b_glu)
    return (v * jax.nn.silu(z.astype(jnp.float32))).astype(u.dtype)


def _ssd_chunked(x, dt, a, b, c):
    bsz, t, h, p = x.shape
    g, n = b.shape[2], b.shape[3]
    r = h // g
    nc = t // CHUNK
    xd = (x * dt[..., None]).reshape(bsz, nc, CHUNK, g, r, p)
    acum = jnp.cumsum((dt * a).reshape(bsz, nc, CHUNK, g, r), axis=2)
    bc = b.reshape(bsz, nc, CHUNK, g, n)
    cc = c.reshape(bsz, nc, CHUNK, g, n)
    causal = jnp.tril(jnp.ones((CHUNK, CHUNK), dtype=bool))
    seg = acum[:, :, :, None] - acum[:, :, None]
    decay = jnp.exp(jnp.where(causal[:, :, None, None], seg, -jnp.inf))
    scores = jnp.einsum('bclgn,bcsgn->bclsg', cc, bc)[..., None] * decay
    y_diag = jnp.einsum('bclsgr,bcsgrp->bclgrp', scores, xd)
    to_end = jnp.exp(acum[:, :, -1:] - acum)
    states = jnp.einsum('bcsgn,bcsgr,bcsgrp->bcgrpn', bc, to_end, xd)
    chunk_decay = jnp.exp(acum[:, :, -1])

    def step(state, inp):
        st, dec = inp
        return state * dec[..., None, None] + st, state

    init = jnp.zeros((bsz, g, r, p, n), x.dtype)
    _, prev = lax.scan(step, init, (jnp.moveaxis(states, 1, 0), jnp.moveaxis(chunk_decay, 1, 0)))
    y_off = jnp.einsum('bclgn,cbgrpn,bclgr->bclgrp', cc, prev, jnp.exp(acum))
    return (y_diag + y_off).reshape(bsz, t, h, p)


def _ssd_branch(xbc, dt_raw, z, conv_w, conv_b, dt_bias, a_log, d, norm_g):
    bsz, t, _ = xbc.shape
    xbc = jax.nn.silu(_causal_dwconv(xbc, conv_w) + conv_b).astype(jnp.float32)
    xs, bs, cs = jnp.split(xbc, [SSD_WIDTH, SSD_WIDTH + SSD_GROUPS * SSD_STATE], axis=-1)
    xs = xs.reshape(bsz, t, SSD_HEADS, SSD_HEAD_DIM)
    bs = bs.reshape(bsz, t, SSD_GROUPS, SSD_STATE)
    cs = cs.reshape(bsz, t, SSD_GROUPS, SSD_STATE)
    dt = jax.nn.softplus(dt_raw.astype(jnp.float32) + dt_bias.astype(jnp.float32))
    a = -jnp.exp(a_log.astype(jnp.float32))
    y = _ssd_chunked(_front_pad(xs), _front_pad(dt), a, _front_pad(bs), _front_pad(cs))[:, META_PAD:]
    y = y + xs * d.astype(jnp.float32)[:, None]
    y = y.reshape(bsz, t, SSD_WIDTH) * jax.nn.silu(z.astype(jnp.float32))
    return _rms_norm(y, norm_g).astype(z.dtype)


def _gated_delta_chunked(q, k, v, beta, g):
    bsz, t, h, dk = k.shape
    dv = v.shape[-1]
    nc = t // CHUNK
    q, k, v, beta, g = (u.reshape(bsz, nc, CHUNK, *u.shape[2:]) for u in (q, k, v, beta, g))
    gcum = jnp.cumsum(g, axis=2)
    causal = jnp.tril(jnp.ones((CHUNK, CHUNK), dtype=bool))
    strict = jnp.tril(jnp.ones((CHUNK, CHUNK), dtype=bool), -1)
    seg = gcum[:, :, :, None, :] - gcum[:, :, None, :, :]
    gamma = jnp.exp(jnp.where(causal[:, :, None], seg, -jnp.inf))
    kk = jnp.einsum('bclhd,bcshd->bclsh', k, k)
    a_mat = jnp.where(strict[:, :, None], kk * gamma * beta[:, :, :, None, :], 0.0)
    rhs = jnp.concatenate([v * beta[..., None], k * (beta * jnp.exp(gcum))[..., None]], axis=-1)
    sol = lax.linalg.triangular_solve(a_mat.transpose(0, 1, 4, 2, 3), rhs.transpose(0, 1, 3, 2, 4),
                                      left_side=True, lower=True, unit_diagonal=True)
    u_c, w_c = sol[..., :dv], sol[..., dv:]
    attn = jnp.einsum('bclhd,bcshd->bchls', q, k) * gamma.transpose(0, 1, 4, 2, 3)
    qg = (q * jnp.exp(gcum)[..., None]).transpose(0, 1, 3, 2, 4)
    kd = (k * jnp.exp(gcum[:, :, -1:] - gcum)[..., None]).transpose(0, 1, 3, 2, 4)
    last = jnp.exp(gcum[:, :, -1])

    def step(state, inp):
        u_i, w_i, qg_i, kd_i, attn_i, last_i = inp
        v_new = u_i - jnp.einsum('bhlk,bhkv->bhlv', w_i, state)
        o = jnp.einsum('bhlk,bhkv->bhlv', qg_i, state) + jnp.einsum('bhls,bhsv->bhlv', attn_i, v_new)
        state = state * last_i[..., None, None] + jnp.einsum('bhlk,bhlv->bhkv', kd_i, v_new)
        return state, o

    init = jnp.zeros((bsz, h, dk, dv), q.dtype)
    xs = tuple(jnp.moveaxis(z, 1, 0) for z in (u_c, w_c, qg, kd, attn, last))
    _, o = lax.scan(step, init, xs)
    return o.transpose(1, 0, 3, 2, 4).reshape(bsz, t, h, dv)


def _gdn_branch(qkv, a_raw, b_raw, z, conv_w, dt_bias, a_log, norm_g):
    bsz, t, _ = qkv.shape
    qkv = jax.nn.silu(_causal_dwconv(qkv, conv_w)).astype(jnp.float32)
    q, k, v = jnp.split(qkv, 3, axis=-1)
    heads = lambda u: u.reshape(bsz, t, GDN_HEADS, GDN_HEAD_DIM)
    q = _l2norm(heads(q)) * GDN_HEAD_DIM ** -0.5
    k = _l2norm(heads(k))
    v = heads(v)
    beta = jax.nn.sigmoid(b_raw.astype(jnp.float32))
    g = -jnp.exp(a_log.astype(jnp.float32)) * jax.nn.softplus(a_raw.astype(jnp.float32) + dt_bias.astype(jnp.float32))
    o = _gated_delta_chunked(_front_pad(q), _front_pad(k), _front_pad(v), _front_pad(beta), _front_pad(g))[:, META_PAD:]
    o = _rms_norm(o, norm_g).reshape(bsz, t, GDN_WIDTH) * jax.nn.silu(z.astype(jnp.float32))
    return o.astype(z.dtype)


def _inv_softplus(y):
    return y + jnp.log(-jnp.expm1(-y))


def _log_uniform(key, shape, lo, hi):
    return jnp.exp(jax.random.uniform(key, shape, jnp.float32, math.log(lo), math.log(hi)))


def setup_inputs(seed: int = 0) -> dict:
    key = jax.random.key(seed)
    ks = jax.random.split(key, 32)

    def nrm(i, shape, scale):
        return scale * jax.random.normal(ks[i], shape, jnp.float32)

    n_idx = jnp.arange(S5_STATE, dtype=jnp.float32)
    s5_shape = (DEPTH, S5_GROUPS, S5_STATE)
    return {
        'x': nrm(0, (BATCH, SEQ, D_MODEL), 1.0),
        'meta': nrm(1, (N_META, D_MODEL), 1.0),
        'ln_in_g': 1.0 + nrm(2, (D_MODEL,), 0.02),
        'ln_in_b': nrm(3, (D_MODEL,), 0.02),
        'w_in': nrm(4, (DEPTH, D_MODEL, IN_WIDTH), D_MODEL ** -0.5),
        's5_a_re': -0.5 + nrm(5, s5_shape, 0.01),
        's5_a_im': math.pi * n_idx + nrm(6, s5_shape, 0.01),
        's5_log_step': jax.random.uniform(ks[7], (DEPTH, S5_GROUPS), jnp.float32, math.log(S5_STEP_MIN), math.log(S5_STEP_MAX)),
        's5_b_re': nrm(8, (DEPTH, S5_GROUPS, S5_STATE, S5_GROUP), S5_GROUP ** -0.5),
        's5_b_im': nrm(9, (DEPTH, S5_GROUPS, S5_STATE, S5_GROUP), S5_GROUP ** -0.5),
        's5_c_re': nrm(10, (DEPTH, S5_GROUPS, S5_GROUP, S5_STATE), S5_STATE ** -0.5),
        's5_c_im': nrm(11, (DEPTH, S5_GROUPS, S5_GROUP, S5_STATE), S5_STATE ** -0.5),
        's5_d': nrm(12, (DEPTH, S5_WIDTH), 1.0),
        's5_w_glu': nrm(13, (DEPTH, S5_WIDTH, S5_WIDTH), S5_WIDTH ** -0.5),
        's5_b_glu': nrm(14, (DEPTH, S5_WIDTH), 0.02),
        'ssd_conv_w': nrm(15, (DEPTH, CONV_K, SSD_CONV_WIDTH), CONV_K ** -0.5),
        'ssd_conv_b': nrm(16, (DEPTH, SSD_CONV_WIDTH), 0.02),
        'ssd_dt_bias': _inv_softplus(_log_uniform(ks[17], (DEPTH, SSD_HEADS), 1e-3, 1e-1)),
        'ssd_a_log': jnp.log(jax.random.uniform(ks[18], (DEPTH, SSD_HEADS), jnp.float32, 1.0, 16.0)),
        'ssd_d': 1.0 + nrm(19, (DEPTH, SSD_HEADS), 0.1),
        'ssd_norm_g': 1.0 + nrm(20, (DEPTH, SSD_WIDTH), 0.02),
        'gdn_conv_w': nrm(21, (DEPTH, CONV_K, 3 * GDN_WIDTH), CONV_K ** -0.5),
        'gdn_dt_bias': _inv_softplus(_log_uniform(ks[22], (DEPTH, GDN_HEADS), 1e-3, 1e-1)),
        'gdn_a_log': jnp.log(jax.random.uniform(ks[23], (DEPTH, GDN_HEADS), jnp.float32, 1.0, 16.0)),
        'gdn_norm_g': 1.0 + nrm(24, (DEPTH, GDN_HEAD_DIM), 0.02),
        'w_branch': nrm(25, (DEPTH, N_BRANCH, BRANCH_WIDTH, D_MODEL), BRANCH_WIDTH ** -0.5 * BETA),
        'b_gate': nrm(26, (DEPTH, N_BRANCH, D_MODEL), 0.02),
        'w_out': nrm(27, (DEPTH, D_MODEL, D_MODEL), D_MODEL ** -0.5 * BETA),
        'ln_g': 1.0 + nrm(28, (DEPTH, D_MODEL), 0.02),
        'ln_b': nrm(29, (DEPTH, D_MODEL), 0.02),
    }


def reference(x, meta, ln_in_g, ln_in_b, w_in, s5_a_re, s5_a_im, s5_log_step, s5_b_re, s5_b_im,
              s5_c_re, s5_c_im, s5_d, s5_w_glu, s5_b_glu, ssd_conv_w, ssd_conv_b, ssd_dt_bias,
              ssd_a_log, ssd_d, ssd_norm_g, gdn_conv_w, gdn_dt_bias, gdn_a_log, gdn_norm_g,
              w_branch, b_gate, w_out, ln_g, ln_b):
    bsz = x.shape[0]
    h = jnp.concatenate([jnp.broadcast_to(meta[None].astype(x.dtype), (bsz, N_META, D_MODEL)), x], axis=1)
    h = _layer_norm(h, ln_in_g, ln_in_b)
    t = h.shape[1]
    points = _split_points()
    for layer in range(DEPTH):
        proj = h @ w_in[layer]
        (s5_u, s5_z, ssd_xbc, ssd_dt, ssd_z, gdn_qkv, gdn_a, gdn_b, gdn_z,
         gate_logits) = jnp.split(proj, points, axis=-1)
        y_a = _s5_branch(s5_u, s5_z, s5_a_re[layer], s5_a_im[layer], s5_log_step[layer], s5_b_re[layer],
                         s5_b_im[layer], s5_c_re[layer], s5_c_im[layer], s5_d[layer], s5_w_glu[layer],
                         s5_b_glu[layer])
        y_b = _ssd_branch(ssd_xbc, ssd_dt, ssd_z, ssd_conv_w[layer], ssd_conv_b[layer], ssd_dt_bias[layer],
                          ssd_a_log[layer], ssd_d[layer], ssd_norm_g[layer])
        y_c = _gdn_branch(gdn_qkv, gdn_a, gdn_b, gdn_z, gdn_conv_w[layer], gdn_dt_bias[layer],
                          gdn_a_log[layer], gdn_norm_g[layer])
        branches = jnp.stack([y_a, y_b, y_c], axis=2)
        outs = jnp.einsum('btnw,nwd->btnd', branches, w_branch[layer])
        gates = jax.nn.sigmoid(gate_logits.reshape(bsz, t, N_BRANCH, D_MODEL) + b_gate[layer])
        merged = jnp.sum(gates * outs, axis=2)
        h = _layer_norm(ALPHA * h + merged @ w_out[layer], ln_g[layer], ln_b[layer])
    return h[:, N_META:]
```

```python
import contextlib
import numpy as np
import concourse.bass as bass
import concourse.mybir as mybir
from concourse.bass_utils import run_bass_kernel_spmd

F32 = mybir.dt.float32
BF16 = mybir.dt.bfloat16
AF = mybir.ActivationFunctionType
ALU = mybir.AluOpType
AX = mybir.AxisListType

D = 1024
SEQ = 4096
NMETA = 16
T = 4112
TP = 4224
NCH = 33
DEPTH = 4
INW = 9752
ALPHA = 8 ** 0.25
EPS = 1e-5
NEG = -30000.0
MAGIC = 12582912.0
TWO_PI = 6.283185
TBLK = [(i * 512, 512) for i in range(8)] + [(4096, 128)]
NCONST = 2050


def make_consts():
    c = np.zeros((128, NCONST), np.float32)
    j = np.arange(128)[:, None]
    l = np.arange(128)[None, :]
    c[:, 0:128] = (j == l)
    c[:, 128:256] = (j <= l)
    c[:, 256:384] = (j > l)
    c[:, 384:512] = 1.0
    strict = np.where(l <= j, NEG, 0.0)
    incl = np.where(l < j, NEG, 0.0)
    c[:, 512:1024] = np.concatenate([strict, incl, strict, incl], axis=1)
    c[:, 1024:1536] = np.concatenate([incl] * 4, axis=1)
    c[:, 1536:2048] = np.arange(512)[None, :]
    c[:, 2048] = (np.arange(128) < 16)
    c[:, 2049] = np.where(np.arange(128) < 16, 0.0, 30000.0)
    return c


QMAP = {}


class Sched:
    def __init__(self, nc, es):
        self.nc = nc
        self.engs = {'pe': nc.tensor, 'act': nc.scalar, 'dve': nc.vector, 'pool': nc.gpsimd, 'sp': nc.sync}
        self.sems = {}
        for e in ['pe', 'act', 'dve', 'pool']:
            self.sems[e] = es.enter_context(nc.semaphore("s_" + e))
        self.cnt = {e: 0 for e in ['pe', 'act', 'dve', 'pool']}
        self.R = {'sp': 8, 'act': 4, 'pool': 4}
        self.dval = {}
        self.dn = {'sp': 0, 'act': 0, 'pool': 0}
        for qn, r in self.R.items():
            for i in range(r):
                self.sems[('d', qn, i)] = es.enter_context(nc.semaphore("sd%s%d" % (qn, i)))
                self.dval[(qn, i)] = 0
        self.waited = {}
        self.lastw = {}
        self.readers = {}
        self.latest = {}

    def _wait(self, eng, ev):
        semid, val, src = ev
        if eng == 'pe' and src == 'pe':
            return
        k = (eng, semid)
        if self.waited.get(k, 0) >= val:
            return
        self.waited[k] = val
        self.engs[eng].wait_ge(self.sems[semid], val)

    def _deps(self, eng, reads, writes):
        for k in reads:
            ev = self.lastw.get(k)
            if ev is not None:
                self._wait(eng, ev)
        for k in writes:
            ev = self.lastw.get(k)
            if ev is not None:
                self._wait(eng, ev)
            for ev in self.readers.get(k, {}).values():
                self._wait(eng, ev)

    def _record(self, ev, reads, writes):
        for k in reads:
            self.readers.setdefault(k, {})[ev[0]] = ev
        for k in writes:
            self.lastw[k] = ev
            self.readers[k] = {}
        self.latest[ev[0]] = ev

    def op(self, eng, fn, reads=(), writes=()):
        self._deps(eng, reads, writes)
        inst = fn(self.engs[eng])
        self.cnt[eng] += 1
        inst.then_inc(self.sems[eng], 1)
        self._record((eng, self.cnt[eng], eng), reads, writes)

    def mm(self, out, pairs, reads=(), writes=()):
        self._deps('pe', reads, writes)
        n = len(pairs)
        inst = None
        for i, (lhsT, rhs) in enumerate(pairs):
            inst = self.nc.tensor.matmul(out, lhsT=lhsT, rhs=rhs, start=(i == 0), stop=(i == n - 1))
        self.cnt['pe'] += 1
        inst.then_inc(self.sems['pe'], 1)
        self._record(('pe', self.cnt['pe'], 'pe'), reads, writes)

    def tr(self, out, in_, ident, reads=(), writes=()):
        self.op('pe', lambda e: e.transpose(out, in_, ident), reads, writes)

    def dma(self, out, in_, reads=(), writes=(), slow=False, q='sp'):
        q = QMAP.get(q, q)
        i = self.dn[q] % self.R[q]
        self.dn[q] += 1
        sid = ('d', q, i)
        if self.dval[(q, i)] > 0:
            self._wait(q, (sid, self.dval[(q, i)], q))
        self._deps(q, reads, writes)
        if slow:
            inst = self.engs[q].dma_start(out=out, in_=in_, allow_slow_non_contiguous=True)
        else:
            inst = self.engs[q].dma_start(out=out, in_=in_)
        self.dval[(q, i)] += 16
        inst.then_inc(self.sems[sid], 16)
        self._record((sid, self.dval[(q, i)], q), reads, writes)

    def barrier(self):
        evs = list(self.latest.values())
        for e in ['pe', 'act', 'dve', 'pool', 'sp']:
            for ev in evs:
                if ev[2] == e and e == 'pe':
                    continue
                self._wait(e, ev)
        self.lastw = {}
        self.readers = {}

    def finish(self):
        for ev in list(self.latest.values()):
            self._wait('sp', ev)

    def act(self, out, in_, func, reads, writes, bias=None, scale=1.0, accum_out=None):
        kw = {}
        if bias is not None:
            kw['bias'] = bias
        if accum_out is not None:
            kw['accum_out'] = accum_out
        self.op('act', lambda e: e.activation(out=out, in_=in_, func=func, scale=scale, **kw), reads, writes)

    def tt(self, out, a, b, op, reads, writes, eng='dve'):
        self.op(eng, lambda e: e.tensor_tensor(out=out, in0=a, in1=b, op=op), reads, writes)

    def ts(self, out, a, s1, op0, reads, writes, s2=None, op1=None, eng='dve'):
        if op1 is None:
            self.op(eng, lambda e: e.tensor_scalar(out=out, in0=a, scalar1=s1, scalar2=None, op0=op0), reads, writes)
        else:
            self.op(eng, lambda e: e.tensor_scalar(out=out, in0=a, scalar1=s1, scalar2=s2, op0=op0, op1=op1), reads, writes)

    def stt(self, out, in0, scalar, in1, op0, op1, reads, writes):
        self.op('dve', lambda e: e.scalar_tensor_tensor(out=out, in0=in0, scalar=scalar, in1=in1, op0=op0, op1=op1), reads, writes)

    def copy(self, eng, out, in_, reads, writes):
        if eng == 'act':
            self.act(out, in_, AF.Copy, reads, writes)
        else:
            self.op(eng, lambda e: e.tensor_copy(out=out, in_=in_), reads, writes)


def bc(ap, shape):
    return ap.broadcast_to(list(shape))


def build(nlayers=DEPTH, debug=False):
    nc = bass.Bass("TRN2", target_bir_lowering=False)

    def din(name, shape):
        return nc.dram_tensor(name, list(shape), F32, kind="ExternalInput").ap()

    x_in = din("x", [SEQ, D])
    meta = din("meta", [NMETA, D])
    ln_in_g = din("ln_in_g", [D])
    ln_in_b = din("ln_in_b", [D])
    w_in = din("w_in", [DEPTH, D, INW])
    s5_a_re = din("s5_a_re", [DEPTH, 48, 64])
    s5_a_im = din("s5_a_im", [DEPTH, 48, 64])
    s5_log_step = din("s5_log_step", [DEPTH, 48])
    s5_b_re = din("s5_b_re", [DEPTH, 48, 64, 16])
    s5_b_im = din("s5_b_im", [DEPTH, 48, 64, 16])
    s5_c_re = din("s5_c_re", [DEPTH, 48, 16, 64])
    s5_c_im = din("s5_c_im", [DEPTH, 48, 16, 64])
    s5_d = din("s5_d", [DEPTH, 768])
    s5_w_glu = din("s5_w_glu", [DEPTH, 768, 768])
    s5_b_glu = din("s5_b_glu", [DEPTH, 768])
    ssd_conv_w = din("ssd_conv_w", [DEPTH, 4, 1280])
    ssd_conv_b = din("ssd_conv_b", [DEPTH, 1280])
    ssd_dt_bias = din("ssd_dt_bias", [DEPTH, 12])
    ssd_a_log = din("ssd_a_log", [DEPTH, 12])
    ssd_d = din("ssd_d", [DEPTH, 12])
    ssd_norm_g = din("ssd_norm_g", [DEPTH, 768])
    gdn_conv_w = din("gdn_conv_w", [DEPTH, 4, 2304])
    gdn_dt_bias = din("gdn_dt_bias", [DEPTH, 6])
    gdn_a_log = din("gdn_a_log", [DEPTH, 6])
    gdn_norm_g = din("gdn_norm_g", [DEPTH, 128])
    w_branch = din("w_branch", [DEPTH, 3, 768, D])
    b_gate = din("b_gate", [DEPTH, 3, D])
    w_out = din("w_out", [DEPTH, D, D])
    ln_g = din("ln_g", [DEPTH, D])
    ln_b = din("ln_b", [DEPTH, D])
    consts_d = din("consts", [128, NCONST])
    y_out = nc.dram_tensor("y", [SEQ, D], F32, kind="ExternalOutput").ap()

    h_dram = nc.dram_tensor("h_dram", [TP, D], F32, kind="Internal").ap()
    hT_dram = nc.dram_tensor("hT_dram", [128, 8, TP], BF16, kind="Internal").ap()
    act_tm = nc.dram_tensor("act_tm", [TP, 3584], F32, kind="Internal").ap()
    ptm = nc.dram_tensor("ptm", [TP, 1560], F32, kind="Internal").ap()
    fm_dram = nc.dram_tensor("fm_dram", [128, 16, TP], F32, kind="Internal").ap()
    Y_dram = nc.dram_tensor("Y_dram", [128, 18, TP], BF16, kind="Internal").ap()
    uz_dram = nc.dram_tensor("uz_dram", [128, 12, TP], F32, kind="Internal").ap()
    v_dram = nc.dram_tensor("v_dram", [128, 6, TP], BF16, kind="Internal").ap()
    dbg = {}
    if debug:
        dbg['act_tm'] = nc.dram_tensor("dbg_act_tm", [TP, 3584], F32, kind="ExternalOutput").ap()
        dbg['ptm'] = nc.dram_tensor("dbg_ptm", [TP, 1560], F32, kind="ExternalOutput").ap()
        dbg['Y'] = nc.dram_tensor("dbg_Y", [128, 18, TP], BF16, kind="ExternalOutput").ap()
        dbg['h'] = nc.dram_tensor("dbg_h", [TP, D], F32, kind="ExternalOutput").ap()

    with contextlib.ExitStack() as es:
        S = Sched(nc, es)
        cst = es.enter_context(nc.sbuf_tensor("cst", [128, NCONST], F32))
        psum = es.enter_context(nc.psum_tensor("psum", [128, 8, 512], F32))
        S.dma(cst[:, :], consts_d[:, :], [], ['cst'])
        ident = cst[:, 0:128]
        triu = cst[:, 128:256]
        su = cst[:, 256:384]
        ones = cst[:, 384:512]
        mgdn = cst[:, 512:1024]
        mssd = cst[:, 1024:1536]
        tidx = cst[:, 1536:2048]
        padmask = cst[:, 2048:2049]
        padbig = cst[:, 2049:2050]

        psp = [0]

        pslo = [0]

        def psalloc(n):
            if psp[0] + n > 8 or psp[0] < pslo[0]:
                psp[0] = pslo[0]
            b = psp[0]
            psp[0] += n
            return b, [('ps', b + i) for i in range(n)]

        def mkps0(lo, hi):
            p = [lo]

            def alloc(n):
                if p[0] + n > hi:
                    p[0] = lo
                b_ = p[0]
                p[0] += n
                return b_, [('ps', b_ + i) for i in range(n)]
            return alloc

        def merge(gens):
            gens = list(gens)
            acc = [0.0] * len(gens)
            alive = list(range(len(gens)))
            while alive:
                i = min(alive, key=lambda a: acc[a])
                try:
                    c = next(gens[i])
                    acc[i] += c
                    yield c
                except StopIteration:
                    alive.remove(i)

        uid = [0]

        def tile(st, shape, dt=F32, name=None):
            uid[0] += 1
            nm = (name or "t") + "_%d" % uid[0]
            return st.enter_context(nc.sbuf_tensor(nm, list(shape), dt))

        def layernorm(st_tiles, src, dst, g_bc, b_bc, rk, wk):
            stats, mv, rstd = st_tiles
            S.op('dve', lambda e: e.bn_stats(out=stats[:, 0, :], in_=src[:, 0:512]), rk, ['ln_stats'])
            S.op('dve', lambda e: e.bn_stats(out=stats[:, 1, :], in_=src[:, 512:1024]), rk, ['ln_stats2'])
            S.op('dve', lambda e: e.bn_aggr(out=mv[:, :], in_=stats[:, :, :]), ['ln_stats', 'ln_stats2'], ['ln_mv'])
            S.act(rstd[:, :], mv[:, 1:2], AF.Sqrt, ['ln_mv'], ['ln_rstd'], bias=EPS)
            S.op('dve', lambda e: e.reciprocal(out=rstd[:, :], in_=rstd[:, :]), ['ln_rstd'], ['ln_rstd'])
            S.ts(dst, src, mv[:, 0:1], ALU.subtract, rk + ['ln_mv', 'ln_rstd'], wk, s2=rstd[:, 0:1], op1=ALU.mult)
            S.tt(dst, dst, g_bc, ALU.mult, wk + ['lnp'], wk)
            S.tt(dst, dst, b_bc, ALU.add, wk + ['lnp'], wk)

        def load_cast(st, dst_bf, src_dram, nk, ncols, wst, key_dst, tag):
            c0 = 0
            i = 0
            while c0 < ncols:
                n = min(512, ncols - c0)
                ws = wst[i % 2]
                wkey = 'wst%d' % (i % 2)
                S.dma(ws[:, 0:nk, 0:n], src_dram[:, c0:c0 + n].rearrange("(k p) c -> p k c", p=128), [], [wkey])
                h = nk // 2
                S.copy('act', dst_bf[:, 0:h, c0:c0 + n], ws[:, 0:h, 0:n], [wkey], [key_dst + '_a%d' % i])
                S.copy('dve', dst_bf[:, h:nk, c0:c0 + n], ws[:, h:nk, 0:n], [wkey], [key_dst + '_b%d' % i])
                c0 += n
                i += 1
            return [key_dst + '_a%d' % j for j in range(i)] + [key_dst + '_b%d' % j for j in range(i)]

        with contextlib.ExitStack() as st:
            xin = [tile(st, [128, D]) for _ in range(2)]
            hts = [tile(st, [128, D]) for _ in range(2)]
            htb = [tile(st, [128, 8, 128], BF16) for _ in range(2)]
            gb = tile(st, [128, D])
            bb = tile(st, [128, D])
            lnt = (tile(st, [128, 2, 6]), tile(st, [128, 2]), tile(st, [128, 1]))
            S.dma(gb[:, :], ln_in_g.partition_broadcast(128), [], ['lnp'])
            S.dma(bb[:, :], ln_in_b.partition_broadcast(128), [], ['lnp'])
            for i in range(NCH):
                b = i % 2
                xk, hk, tk = 'xin%d' % b, 'hts%d' % b, 'htb%d' % b
                xt = xin[b]
                if i == 0:
                    S.dma(xt[0:16, :], meta[:, :], [], [xk])
                    S.dma(xt[16:128, :], x_in[0:112, :], [], [xk + 'b'])
                    rk = [xk, xk + 'b']
                elif i < 32:
                    S.dma(xt[:, :], x_in[128 * i - 16:128 * i + 112, :], [xk + 'b'], [xk])
                    rk = [xk]
                else:
                    S.op('pool', lambda e: e.memset(xt[:, :], 0.0), [], [xk])
                    S.dma(xt[0:16, :], x_in[4080:4096, :], [], [xk])
                    rk = [xk]
                layernorm(lnt, xt[:, :], hts[b][:, :], gb[:, :], bb[:, :], rk, [hk])
                S.dma(h_dram[128 * i:128 * i + 128, :], hts[b][:, :], [hk], [('h', i)], q='act')
                pb, pk = psalloc(2)
                for k in range(8):
                    S.tr(psum[:, pb + k // 4, (k % 4) * 128:(k % 4) * 128 + 128], hts[b][:, 128 * k:128 * k + 128], ident,
                         [hk, 'cst'], [pk[k // 4]])
                S.copy('act', htb[b][:, 0:4, :], psum[:, pb, :].rearrange("p (a c) -> p a c", a=4), [pk[0]], [tk])
                S.copy('dve', htb[b][:, 4:8, :], psum[:, pb + 1, :].rearrange("p (a c) -> p a c", a=4), [pk[1]], [tk + 'b'])
                S.dma(hT_dram[:, :, 128 * i:128 * i + 128], htb[b][:, :, :], [tk, tk + 'b'], [('hT', i)], q='act')
        S.barrier()

        for l in range(nlayers):
            last_layer = (l == nlayers - 1) and not debug
            with contextlib.ExitStack() as st:
                hT = tile(st, [128, 8, TP], BF16)
                for k in range(8):
                    S.dma(hT[:, k, :], hT_dram[:, k, :], [], ['hTk%d' % k])
                hTk = ['hTk%d' % k for k in range(8)]
                st_outer = st
                cw = tile(st_outer, [128, 28, 4])
                cb = tile(st_outer, [128, 28])
                st = st_outer.enter_context(contextlib.ExitStack())
                wst = [tile(st, [128, 8, 512]) for _ in range(2)]
                wb = [tile(st, [128, 8, 512], BF16) for _ in range(2)]
                ost = [tile(st, [128, 512]) for _ in range(4)]
                S.op('pool', lambda e: e.memset(cb[:, :], 0.0), [], ['cb'])
                for tap in range(4):
                    S.dma(cw[:, 0:10, tap], ssd_conv_w[l, tap, :].rearrange("(j p) -> p j", p=128), [], ['cw'], slow=True)
                    S.dma(cw[:, 10:28, tap], gdn_conv_w[l, tap, :].rearrange("(j p) -> p j", p=128), [], ['cw'], slow=True)
                S.dma(cb[:, 0:10], ssd_conv_b[l, :].rearrange("(j p) -> p j", p=128), [], ['cb'], slow=True)
                blocks = [(2816, 512, 0), (3328, 268, 512), (5900, 512, 780), (6412, 268, 1292)]
                on = 0
                for bi, (c0, n, p0) in enumerate(blocks):
                    ws = wst[bi % 2]
                    wbb = wb[bi % 2]
                    S.dma(ws[:, :, 0:n], w_in[l, :, c0:c0 + n].rearrange("(k p) c -> p k c", p=128), [], ['wst%d' % (bi % 2)])
                    S.copy('act', wbb[:, 0:4, 0:n], ws[:, 0:4, 0:n], ['wst%d' % (bi % 2)], ['wb%da' % (bi % 2)])
                    S.copy('dve', wbb[:, 4:8, 0:n], ws[:, 4:8, 0:n], ['wst%d' % (bi % 2)], ['wb%db' % (bi % 2)])
                    wk = ['wb%da' % (bi % 2), 'wb%db' % (bi % 2)]
                    for i in range(NCH):
                        pb, pk = psalloc(1)
                        S.mm(psum[:, pb, 0:n], [(hT[:, k, 128 * i:128 * i + 128], wbb[:, k, 0:n]) for k in range(8)],
                             hTk + wk, pk)
                        o = ost[on % 4]
                        ok = 'ost%d' % (on % 4)
                        zs = 12 if bi in (0, 2) else 0
                        S.act(o[:, 0:n], psum[:, pb, 0:n], AF.Silu, pk, [ok])
                        if zs:
                            S.copy('dve', o[:, 0:zs], psum[:, pb, 0:zs], pk + [ok], [ok])
                        S.dma(ptm[128 * i:128 * i + 128, p0:p0 + n], o[:, 0:n], [ok], [('ptm', i, bi)], q='act')
                        on += 1
                S.barrier()
                st.close()
                st = st_outer
                xf = [tile(st, [128, 3 + TP]) for _ in range(2)]
                acc = [tile(st, [128, TP]) for _ in range(2)]
                rsb = [tile(st, [128, TP]) for _ in range(2)]
                wst2 = [tile(st, [128, 8, 128]) for _ in range(2)]
                wb2 = [tile(st, [128, 8, 128], BF16) for _ in range(2)]
                tstb = [[tile(st, [128, 4, 128]) for _ in range(2)] for _ in range(2)]
                for b in range(2):
                    S.op('pool', lambda e: e.memset(xf[b][:, 0:3], 0.0), [], ['xfz%d' % b])
                tnc = [0, 0]
                psP = [mkps0(0, 4), mkps0(4, 8)]
                def stA_hdr(jj):
                    b = jj % 2
                    j = jj - 12
                    if jj < 12:
                        c0 = 128 * jj
                    else:
                        c0 = 1536 + 128 * j if j < 10 else 3596 + 128 * (j - 10)
                    return b, j, c0

                def projP(jj):
                    b, j, c0 = stA_hdr(jj)
                    S.dma(wst2[b][:, :, :], w_in[l, :, c0:c0 + 128].rearrange("(k p) c -> p k c", p=128), [], ['wst2%d' % b])
                    S.copy('act', wb2[b][:, :, :], wst2[b][:, :, :], ['wst2%d' % b], ['wb2%d' % b])
                    xk = ['xf%d_%d' % (b, q) for q in range(9)]
                    for q, (t0, n) in enumerate(TBLK):
                        pb, pk = psP[b](1)
                        S.mm(psum[:, pb, 0:n], [(wb2[b][:, k, :], hT[:, k, t0:t0 + n]) for k in range(8)],
                             hTk + ['wb2%d' % b], pk)
                        S.copy('act' if q % 2 == 0 else 'dve', xf[b][:, 3 + t0:3 + t0 + n], psum[:, pb, 0:n], pk, [xk[q]])
                        yield 2

                def restC(jj):
                    b, j, c0 = stA_hdr(jj)
                    xk = ['xf%d_%d' % (b, q) for q in range(9)]
                    if jj < 12:
                        if jj >= 6:
                            S.act(xf[b][:, 3:3 + TP], xf[b][:, 3:3 + TP], AF.Silu, xk, xk)
                        S.dma(uz_dram[:, jj, :], xf[b][:, 3:3 + TP], xk, [('uz', jj)], q='act')
                        yield 2
                        return
                    ak = 'acc%d' % b
                    a = acc[b]
                    S.act(a[:, :], xf[b][:, 3:3 + TP], AF.Identity, xk + ['cw'], [ak], scale=cw[:, j, 3:4])
                    yield 4
                    for tap in (2, 1, 0):
                        S.stt(a[:, :], xf[b][:, tap:tap + TP], cw[:, j, tap:tap + 1], a[:, :], ALU.mult, ALU.add,
                              xk + ['xfz%d' % b, ak, 'cw'], [ak])
                        yield 5
                    S.act(a[:, :], a[:, :], AF.Silu, [ak, 'cb'], [ak], bias=cb[:, j:j + 1])
                    yield 4
                    if 10 <= j < 22:
                        S.act(rsb[b][:, :], a[:, :], AF.Square, [ak], ['rs%d' % b])
                        yield 4
                        for q, (t0, n) in enumerate(TBLK):
                            pb, pk = psP[b](1)
                            S.mm(psum[:, pb, 0:n], [(ones, rsb[b][:, t0:t0 + n])], ['rs%d' % b, 'cst'], pk)
                            S.act(xf[b][:, 3 + t0:3 + t0 + n], psum[:, pb, 0:n], AF.Ln, pk, [xk[q]], bias=1e-6)
                            yield 1
                        S.act(rsb[b][:, :], xf[b][:, 3:3 + TP], AF.Exp, xk, ['rs%d' % b], scale=-0.5)
                        yield 4
                        if j < 16:
                            S.stt(a[:, :], a[:, :], 128.0 ** -0.5, rsb[b][:, :], ALU.mult, ALU.mult, [ak, 'rs%d' % b], [ak])
                        else:
                            S.tt(a[:, :], a[:, :], rsb[b][:, :], ALU.mult, [ak, 'rs%d' % b], [ak])
                    if 6 <= j < 10:
                        S.dma(fm_dram[:, j - 6, :], a[:, :], [ak], [('fm', j)], q='act')
                    elif 10 <= j < 22:
                        S.dma(fm_dram[:, 4 + (j - 10), :], a[:, :], [ak], [('fm', j)], q='act')
                    for i0 in range(0, NCH, 4):
                        ni = min(4, NCH - i0)
                        pb, pk = psP[b](1)
                        for ii in range(ni):
                            i = i0 + ii
                            S.tr(psum[:, pb, ii * 128:ii * 128 + 128], a[:, 128 * i:128 * i + 128], ident, [ak, 'cst'], pk)
                        tb_ = tstb[b][tnc[b] % 2]
                        tk = 'tst%d_%d' % (b, tnc[b] % 2)
                        S.copy('act' if tnc[b] % 2 == 0 else 'dve', tb_[:, 0:ni, :],
                               psum[:, pb, 0:ni * 128].rearrange("p (a c) -> p a c", a=ni), pk, [tk])
                        S.dma(act_tm[128 * i0:128 * (i0 + ni), 128 * j:128 * j + 128].rearrange("(a p) c -> p a c", p=128),
                              tb_[:, 0:ni, :], [tk], [('act', j, i0)], q='act')
                        tnc[b] += 1
                        yield 2

                def stA_gen(par):
                    for jj in range(par, 40, 2):
                        yield from projP(jj)
                        yield from restC(jj)

                for _ in merge([stA_gen(0), stA_gen(1)]):
                    pass
            S.barrier()

            with contextlib.ExitStack() as st:
                def mkps(lo, hi):
                    p = [lo]

                    def alloc(n):
                        if p[0] + n > hi:
                            p[0] = lo
                        b = p[0]
                        p[0] += n
                        return b, [('ps', b + i) for i in range(n)]
                    return alloc
                psG = mkps(3, 6)
                psS = mkps(6, 8)

                dtb = tile(st, [128, 12]); ssdA = tile(st, [128, 12]); ssdD = tile(st, [128, 12]); ngb = tile(st, [128, 768])
                gdtb = tile(st, [128, 6]); gA = tile(st, [128, 6]); gng = tile(st, [128, 128])
                ST = tile(st, [128, 12, 64]); GS = tile(st, [128, 6, 128])
                C = ['cdp']
                S.dma(dtb[:, :], ssd_dt_bias[l].partition_broadcast(128), [], C)
                S.dma(ssdA[:, :], ssd_a_log[l].partition_broadcast(128), [], C)
                S.dma(ssdD[:, :], ssd_d[l].partition_broadcast(128), [], C)
                S.dma(ngb[:, :], ssd_norm_g[l].partition_broadcast(128), [], C)
                S.dma(gdtb[:, :], gdn_dt_bias[l].partition_broadcast(128), [], C)
                S.dma(gA[:, :], gdn_a_log[l].partition_broadcast(128), [], C)
                S.dma(gng[:, :], gdn_norm_g[l].partition_broadcast(128), [], C)
                S.act(ssdA[:, :], ssdA[:, :], AF.Exp, C, C)
                S.ts(ssdA[:, :], ssdA[:, :], -1.0, ALU.mult, C, C)
                S.act(gA[:, :], gA[:, :], AF.Exp, C, C)
                S.ts(gA[:, :], gA[:, :], -1.0, ALU.mult, C, C)
                S.op('pool', lambda e: e.memset(ST[:, :, :], 0.0), [], ['ST'])
                S.op('pool', lambda e: e.memset(GS[:, :, :], 0.0), [], ['GS'])

                A_ = [tile(st, [128, 3584]) for _ in range(1)]
                P_ = [tile(st, [128, 1560]) for _ in range(1)]
                FM = [tile(st, [128, 16, 128]) for _ in range(1)]
                tl = {}

                def mk(name, shape, dt=F32):
                    tl[name] = tile(st, shape, dt, name)

                for nm in ['dt', 'da', 'acum', 'eac', 'cd', 'toe', 'ssq', 'rstd', 'bet', 'lnb', 'gg', 'gcum', 'egc', 'eend',
                           'glast', 'bge', 'oss', 'orst']:
                    mk(nm, [128, 12])
                for nm in ['SD', 'G', 'dec', 'Gall', 'E', 'AA']:
                    mk(nm, [128, 1536])
                for nm in ['xd', 'xdte', 'yo', 'xD', 'szs', 'Bm0', 'Bm1', 'BT0', 'BT1', 'Pinv', 'bv', 'wkk', 'nwT', 'vn', 'qs',
                           'kd', 'osq']:
                    mk(nm, [128, 768])
                mk('ob0', [128, 6, 128], BF16)
                mk('ob1', [128, 6, 128], BF16)

                def h3(ap, h):
                    return ap.rearrange("p (h d) -> p h d", h=h)


                def ssd_gen(c, A, Pm, F, Ak, Pk, Fk):

                    dt, da, acum, eac, cd, toe = tl['dt'], tl['da'], tl['acum'], tl['eac'], tl['cd'], tl['toe']
                    S.tt(dt[:, :], Pm[:, 0:12], dtb[:, :], ALU.add, [Pk] + C, ['dt'])
                    S.act(dt[:, :], dt[:, :], AF.Exp, ['dt'], ['dt'])
                    S.act(dt[:, :], dt[:, :], AF.Ln, ['dt'], ['dt'], bias=1.0)
                    yield 3
                    S.tt(da[:, :], dt[:, :], ssdA[:, :], ALU.mult, ['dt'] + C, ['da'])
                    pb, pk = psS(1)
                    S.mm(psum[:, pb, 0:12], [(triu, da[:, :])], ['da', 'cst'], pk)
                    S.mm(psum[:, pb, 16:28], [(ones, da[:, :])], ['da', 'cst'], pk)
                    yield 3
                    S.copy('dve', acum[:, :], psum[:, pb, 0:12], pk, ['acum'])
                    S.act(eac[:, :], psum[:, pb, 0:12], AF.Exp, pk, ['eac'])
                    S.act(cd[:, :], psum[:, pb, 16:28], AF.Exp, pk, ['cd'])
                    yield 3
                    S.tt(toe[:, :], psum[:, pb, 16:28], acum[:, :], ALU.subtract, pk + ['acum'], ['toe'])
                    S.act(toe[:, :], toe[:, :], AF.Exp, ['toe'], ['toe'])
                    xd, xdte, SD, yo, G, dec, xD, szs = tl['xd'], tl['xdte'], tl['SD'], tl['yo'], tl['G'], tl['dec'], tl['xD'], tl['szs']
                    Ax = h3(A[:, 0:768], 12)
                    S.tt(h3(xd[:, :], 12), Ax, bc(dt[:, :].unsqueeze(2), [128, 12, 64]), ALU.mult, [Ak, 'dt'], ['xd'])
                    yield 3
                    S.tt(h3(xdte[:, :], 12), h3(xd[:, :], 12), bc(toe[:, :].unsqueeze(2), [128, 12, 64]), ALU.mult, ['xd', 'toe'],
                         ['xdte'], eng='pool')
                    S.tt(h3(G[:, :], 12), bc(triu.unsqueeze(1), [128, 12, 128]), bc(da[:, :].unsqueeze(2), [128, 12, 128]), ALU.mult,
                         ['cst', 'da'], ['G'])
                    for q in range(3):
                        pd, pkd = psS(1)
                        S.mm(psum[:, pd, :], [(su, G[:, 512 * q:512 * q + 512]), (ident, mssd)], ['G', 'cst'], pkd)
                        S.act(dec[:, 512 * q:512 * q + 512], psum[:, pd, :], AF.Exp, pkd, ['dec'])
                    yield 3
                    pc, pkc = psS(1)
                    for g in range(2):
                        S.mm(psum[:, pc, 128 * g:128 * g + 128], [(F[:, g, :], F[:, 2 + g, :])], [Fk], pkc)
                    S.tt(SD[:, :].rearrange("p (g h d) -> p g h d", g=2, h=6),
                         dec[:, :].rearrange("p (g h d) -> p g h d", g=2, h=6),
                         bc(psum[:, pc, 0:256].rearrange("p (g d) -> p g d", g=2).unsqueeze(2), [128, 2, 6, 128]),
                         ALU.mult, ['dec'] + pkc, ['SD'])
                    po, pko = psS(2)
                    for g in range(2):
                        S.mm(psum[:, po + g, 0:384], [(F[:, 2 + g, :], ST[:, 6 * g:6 * g + 6, :])], [Fk, 'ST'], [pko[g]])
                    S.tt(yo[:, :].rearrange("p (g h d) -> p g h d", g=2, h=6),
                         psum[:, po:po + 2, 0:384].rearrange("p g (h d) -> p g h d", h=6),
                         bc(eac[:, :].rearrange("p (g h) -> p g h", g=2).unsqueeze(3), [128, 2, 6, 64]),
                         ALU.mult, pko + ['eac'], ['yo'])
                    py, pky = psS(2)
                    for h in range(12):
                        S.mm(psum[:, py + h // 8, (h % 8) * 64:(h % 8) * 64 + 64],
                             [(SD[:, 128 * h:128 * h + 128], xd[:, 64 * h:64 * h + 64])], ['SD', 'xd'], [pky[h // 8]])
                    S.tt(yo[:, 0:512], yo[:, 0:512], psum[:, py, :], ALU.add, ['yo', pky[0]], ['yo'])
                    yield 3
                    S.tt(yo[:, 512:768], yo[:, 512:768], psum[:, py + 1, 0:256], ALU.add, ['yo', pky[1]], ['yo'])
                    S.tt(h3(xD[:, :], 12), Ax, bc(ssdD[:, :].unsqueeze(2), [128, 12, 64]), ALU.mult, [Ak] + C, ['xD'], eng='pool')
                    S.tt(yo[:, :], yo[:, :], xD[:, :], ALU.add, ['yo', 'xD'], ['yo'])
                    yield 3
                    pn, pkn = psS(2)
                    for g in range(2):
                        S.mm(psum[:, pn + g, 0:384], [(A[:, 768 + 128 * g:768 + 128 * g + 128], xdte[:, 384 * g:384 * g + 384])],
                             [Ak, 'xdte'], [pkn[g]])
                    S.tt(ST[:, :, :], ST[:, :, :], bc(cd[:, :].unsqueeze(2), [128, 12, 64]), ALU.mult, ['ST', 'cd'], ['ST'])
                    S.tt(ST[:, :, :].rearrange("p (g h) d -> p g (h d)", g=2), ST[:, :, :].rearrange("p (g h) d -> p g (h d)", g=2),
                         psum[:, pn:pn + 2, 0:384], ALU.add, ['ST'] + pkn, ['ST'])
                    ssq, rstd = tl['ssq'], tl['rstd']
                    yield 3
                    S.tt(yo[:, :], yo[:, :], Pm[:, 12:780], ALU.mult, ['yo', Pk], ['yo'])
                    S.act(szs[:, :], yo[:, :], AF.Square, ['yo'], ['szs'])
                    S.op('dve', lambda e: e.tensor_reduce(out=ssq[:, 0:1], in_=szs[:, :], axis=AX.X, op=ALU.add), ['szs'], ['ssq'])
                    yield 3
                    S.act(rstd[:, 0:1], ssq[:, 0:1], AF.Ln, ['ssq'], ['rstd'], scale=1.0 / 768, bias=EPS)
                    S.act(rstd[:, 0:1], rstd[:, 0:1], AF.Exp, ['rstd'], ['rstd'], scale=-0.5)
                    S.stt(yo[:, :], yo[:, :], rstd[:, 0:1], ngb[:, :], ALU.mult, ALU.mult, ['yo', 'rstd'] + C, ['yo'])
                    yield 3
                    pt, pkt = psS(2)
                    for k in range(6):
                        S.tr(psum[:, pt + k // 4, (k % 4) * 128:(k % 4) * 128 + 128], yo[:, 128 * k:128 * k + 128], ident,
                             ['yo', 'cst'], [pkt[k // 4]])
                    ob = tl['ob0']
                    S.copy('act', ob[:, 0:4, :], psum[:, pt, :].rearrange("p (a c) -> p a c", a=4), [pkt[0]], ['ob0'])
                    S.copy('act', ob[:, 4:6, :], psum[:, pt + 1, 0:256].rearrange("p (a c) -> p a c", a=2), [pkt[1]], ['ob0'])
                    S.dma(Y_dram[:, 6:12, 128 * c:128 * c + 128], ob[:, :, :], ['ob0'], [('Yb', c)], q='act')
                    yield 3


                def gdn_gen(c, A, Pm, F, Ak, Pk, Fk):

                    bet, lnb, gg, gcum, egc, eend, glast, bge = (tl['bet'], tl['lnb'], tl['gg'], tl['gcum'], tl['egc'], tl['eend'],
                                                                 tl['glast'], tl['bge'])
                    S.act(bet[:, 0:6], Pm[:, 786:792], AF.Exp, [Pk], ['bet'], scale=-1.0)
                    S.act(lnb[:, 0:6], bet[:, 0:6], AF.Ln, ['bet'], ['lnb'], bias=1.0)
                    S.ts(bet[:, 0:6], bet[:, 0:6], 1.0, ALU.add, ['bet', 'lnb'], ['bet'])
                    S.op('dve', lambda e: e.reciprocal(out=bet[:, 0:6], in_=bet[:, 0:6]), ['bet'], ['bet'])
                    if c == NCH - 1:
                        S.ts(bet[:, 0:6], bet[:, 0:6], padmask, ALU.mult, ['bet', 'cst'], ['bet'])
                        S.ts(lnb[:, 0:6], lnb[:, 0:6], padmask, ALU.mult, ['lnb', 'cst'], ['lnb'], s2=padbig, op1=ALU.add)
                    S.tt(gg[:, 0:6], Pm[:, 780:786], gdtb[:, :], ALU.add, [Pk] + C, ['gg'])
                    yield 3
                    S.act(gg[:, 0:6], gg[:, 0:6], AF.Exp, ['gg'], ['gg'])
                    S.act(gg[:, 0:6], gg[:, 0:6], AF.Ln, ['gg'], ['gg'], bias=1.0)
                    S.tt(gg[:, 0:6], gg[:, 0:6], gA[:, :], ALU.mult, ['gg'] + C, ['gg'])
                    yield 3
                    pb, pk = psG(1)
                    S.mm(psum[:, pb, 0:6], [(triu, gg[:, 0:6])], ['gg', 'cst'], pk)
                    S.mm(psum[:, pb, 8:14], [(ones, gg[:, 0:6])], ['gg', 'cst'], pk)
                    S.copy('dve', gcum[:, 0:6], psum[:, pb, 0:6], pk, ['gcum'])
                    yield 3
                    S.act(egc[:, 0:6], psum[:, pb, 0:6], AF.Exp, pk, ['egc'])
                    S.act(glast[:, 0:6], psum[:, pb, 8:14], AF.Exp, pk, ['glast'])
                    S.tt(eend[:, 0:6], psum[:, pb, 8:14], gcum[:, 0:6], ALU.subtract, pk + ['gcum'], ['eend'])
                    yield 3
                    S.act(eend[:, 0:6], eend[:, 0:6], AF.Exp, ['eend'], ['eend'])
                    S.tt(bge[:, 0:6], bet[:, 0:6], egc[:, 0:6], ALU.mult, ['bet', 'egc'], ['bge'])
                    Gall, E, AA = tl['Gall'], tl['E'], tl['AA']
                    G4 = Gall[:, :].rearrange("p (h a d) -> p h a d", h=6, a=2)
                    S.tt(G4[:, :, 1, :], bc(triu.unsqueeze(1), [128, 6, 128]), bc(gg[:, 0:6].unsqueeze(2), [128, 6, 128]), ALU.mult,
                         ['cst', 'gg'], ['Gall1'])
                    yield 3
                    S.tt(G4[:, :, 0, :], bc(ident.unsqueeze(1), [128, 6, 128]), bc(lnb[:, 0:6].unsqueeze(2), [128, 6, 128]), ALU.mult,
                         ['cst', 'lnb'], ['Gall0'], eng='pool')
                    S.tt(G4[:, :, 0, :], G4[:, :, 1, :], G4[:, :, 0, :], ALU.subtract, ['Gall0', 'Gall1'], ['Gall0'])
                    pe_, pke = psG(3)
                    for q in range(3):
                        S.mm(psum[:, pe_ + q, :], [(su, Gall[:, 512 * q:512 * q + 512]), (ident, mgdn)], ['Gall0', 'Gall1', 'cst'],
                             [pke[q]])
                    S.act(E[:, :], psum[:, pe_:pe_ + 3, :].rearrange("p a c -> p (a c)"), AF.Exp, pke, ['E'])
                    yield 3
                    pkk, pkkk = psG(3)
                    for h in range(6):
                        bnk = pkk + h // 2
                        off = (h % 2) * 256
                        S.mm(psum[:, bnk, off:off + 128], [(F[:, 10 + h, :], F[:, 10 + h, :])], [Fk], [pkkk[h // 2]])
                        S.mm(psum[:, bnk, off + 128:off + 256], [(F[:, 10 + h, :], F[:, 4 + h, :])], [Fk], [pkkk[h // 2]])
                    S.tt(AA[:, :], E[:, :], psum[:, pkk:pkk + 3, :].rearrange("p a c -> p (a c)"), ALU.mult, ['E'] + pkkk, ['AA'])
                    AA4 = AA[:, :].rearrange("p (h a d) -> p h a d", h=6, a=2)
                    Bm = [tl['Bm0'], tl['Bm1']]
                    BT = [tl['BT0'], tl['BT1']]
                    Pv = tl['Pinv']
                    S.ts(h3(Bm[0][:, :], 6), AA4[:, :, 0, :], -1.0, ALU.mult, ['AA'], ['Bm0'])
                    pt, pkt = psG(2)
                    for h in range(6):
                        S.tr(psum[:, pt + h // 4, (h % 4) * 128:(h % 4) * 128 + 128], Bm[0][:, 128 * h:128 * h + 128], ident,
                             ['Bm0', 'cst'], [pkt[h // 4]])
                    S.copy('act', BT[0][:, 0:512], psum[:, pt, :], [pkt[0]], ['BT0'])
                    yield 3
                    S.copy('act', BT[0][:, 512:768], psum[:, pt + 1, 0:256], [pkt[1]], ['BT0'])
                    S.tt(h3(Pv[:, :], 6), h3(Bm[0][:, :], 6), bc(ident.unsqueeze(1), [128, 6, 128]), ALU.add, ['Bm0', 'cst'], ['Pinv'])
                    cur = 0
                    for lev in range(1, 7):
                        nxt = 1 - cur
                        bk, btk = 'Bm%d' % cur, 'BT%d' % cur
                        nbk, nbtk = 'Bm%d' % nxt, 'BT%d' % nxt
                        p2, pk2 = psG(2)
                        for h in range(6):
                            S.mm(psum[:, p2 + h // 4, (h % 4) * 128:(h % 4) * 128 + 128],
                                 [(Bm[cur][:, 128 * h:128 * h + 128], BT[cur][:, 128 * h:128 * h + 128])], [bk, btk], [pk2[h // 4]])
                        S.copy('act', BT[nxt][:, 0:512], psum[:, p2, :], [pk2[0]], [nbtk])
                        S.copy('act', BT[nxt][:, 512:768], psum[:, p2 + 1, 0:256], [pk2[1]], [nbtk])
                        if lev < 6:
                            p1 = 5
                            pk1 = [('ps', 5), ('ps', 5)]
                            for half, (h0, h1) in enumerate(((0, 4), (4, 6))):
                                for h in range(h0, h1):
                                    S.mm(psum[:, 5, (h - h0) * 128:(h - h0) * 128 + 128],
                                         [(BT[cur][:, 128 * h:128 * h + 128], Bm[cur][:, 128 * h:128 * h + 128])], [bk, btk],
                                         [('ps', 5)])
                                S.copy('dve', Bm[nxt][:, 128 * h0:128 * h1], psum[:, 5, 0:128 * (h1 - h0)], [('ps', 5)], [nbk])
                        yield 2
                        p3, pk3 = psG(2)
                        for h in range(6):
                            S.mm(psum[:, p3 + h // 4, (h % 4) * 128:(h % 4) * 128 + 128],
                                 [(BT[nxt][:, 128 * h:128 * h + 128], Pv[:, 128 * h:128 * h + 128])], [nbtk, 'Pinv'], [pk3[h // 4]])
                        S.tt(Pv[:, 0:512], Pv[:, 0:512], psum[:, p3, :], ALU.add, ['Pinv', pk3[0]], ['Pinv'])
                        S.tt(Pv[:, 512:768], Pv[:, 512:768], psum[:, p3 + 1, 0:256], ALU.add, ['Pinv', pk3[1]], ['Pinv'])
                        yield 2
                        cur = nxt
                    bv, wkk, nwT, vn, qs, kd, osq = tl['bv'], tl['wkk'], tl['nwT'], tl['vn'], tl['qs'], tl['kd'], tl['osq']
                    Av = h3(A[:, 2816:3584], 6)
                    Akk = h3(A[:, 2048:2816], 6)
                    S.tt(h3(bv[:, :], 6), Av, bc(bet[:, 0:6].unsqueeze(2), [128, 6, 128]), ALU.mult, [Ak, 'bet'], ['bv'], eng='pool')
                    S.tt(h3(wkk[:, :], 6), Akk, bc(bge[:, 0:6].unsqueeze(2), [128, 6, 128]), ALU.mult, [Ak, 'bge'], ['wkk'], eng='pool')
                    S.tt(h3(kd[:, :], 6), Akk, bc(eend[:, 0:6].unsqueeze(2), [128, 6, 128]), ALU.mult, [Ak, 'eend'], ['kd'], eng='pool')
                    yield 3
                    pw, pkw = psG(2)
                    for h in range(6):
                        S.mm(psum[:, pw + h // 4, (h % 4) * 128:(h % 4) * 128 + 128],
                             [(wkk[:, 128 * h:128 * h + 128], Pv[:, 128 * h:128 * h + 128])], ['wkk', 'Pinv'], [pkw[h // 4]])
                    S.act(nwT[:, 0:512], psum[:, pw, :], AF.Copy, [pkw[0]], ['nwT'], scale=-1.0)
                    S.act(nwT[:, 512:768], psum[:, pw + 1, 0:256], AF.Copy, [pkw[1]], ['nwT'], scale=-1.0)
                    pv, pkv = psG(2)
                    for h in range(6):
                        S.mm(psum[:, pv + h // 4, (h % 4) * 128:(h % 4) * 128 + 128],
                             [(Pv[:, 128 * h:128 * h + 128], bv[:, 128 * h:128 * h + 128]),
                              (nwT[:, 128 * h:128 * h + 128], GS[:, h, :])], ['Pinv', 'bv', 'nwT', 'GS'], [pkv[h // 4]])
                    S.copy('dve', vn[:, 0:512], psum[:, pv, :], [pkv[0]], ['vn'])
                    yield 3
                    S.copy('dve', vn[:, 512:768], psum[:, pv + 1, 0:256], [pkv[1]], ['vn'])
                    pq, pkq = psG(2)
                    for h in range(6):
                        S.mm(psum[:, pq + h // 4, (h % 4) * 128:(h % 4) * 128 + 128], [(F[:, 4 + h, :], GS[:, h, :])], [Fk, 'GS'],
                             [pkq[h // 4]])
                    S.tt(h3(qs[:, 0:512], 4), h3(psum[:, pq, :], 4), bc(egc[:, 0:4].unsqueeze(2), [128, 4, 128]), ALU.mult,
                         [pkq[0], 'egc'], ['qs'])
                    S.tt(h3(qs[:, 512:768], 2), h3(psum[:, pq + 1, 0:256], 2), bc(egc[:, 4:6].unsqueeze(2), [128, 2, 128]), ALU.mult,
                         [pkq[1], 'egc'], ['qs'])
                    yield 3
                    pa, pka = psG(2)
                    for h in range(6):
                        S.mm(psum[:, pa + h // 4, (h % 4) * 128:(h % 4) * 128 + 128],
                             [(AA[:, 256 * h + 128:256 * h + 256], vn[:, 128 * h:128 * h + 128])], ['AA', 'vn'], [pka[h // 4]])
                    S.tt(qs[:, 0:512], qs[:, 0:512], psum[:, pa, :], ALU.add, ['qs', pka[0]], ['qs'])
                    S.tt(qs[:, 512:768], qs[:, 512:768], psum[:, pa + 1, 0:256], ALU.add, ['qs', pka[1]], ['qs'])
                    pu, pku = psG(2)
                    for h in range(6):
                        S.mm(psum[:, pu + h // 4, (h % 4) * 128:(h % 4) * 128 + 128],
                             [(kd[:, 128 * h:128 * h + 128], vn[:, 128 * h:128 * h + 128])], ['kd', 'vn'], [pku[h // 4]])
                    S.tt(GS[:, :, :], GS[:, :, :], bc(glast[:, 0:6].unsqueeze(2), [128, 6, 128]), ALU.mult, ['GS', 'glast'], ['GS'])
                    yield 3
                    S.tt(GS[:, 0:4, :].rearrange("p h d -> p (h d)"), GS[:, 0:4, :].rearrange("p h d -> p (h d)"), psum[:, pu, :],
                         ALU.add, ['GS', pku[0]], ['GS'])
                    S.tt(GS[:, 4:6, :].rearrange("p h d -> p (h d)"), GS[:, 4:6, :].rearrange("p h d -> p (h d)"),
                         psum[:, pu + 1, 0:256], ALU.add, ['GS', pku[1]], ['GS'])
                    oss, orst = tl['oss'], tl['orst']
                    S.act(osq[:, :], qs[:, :], AF.Square, ['qs'], ['osq'])
                    yield 3
                    S.op('dve', lambda e: e.tensor_reduce(out=oss[:, 0:6], in_=h3(osq[:, :], 6), axis=AX.X, op=ALU.add),
                         ['osq'], ['oss'])
                    S.act(orst[:, 0:6], oss[:, 0:6], AF.Ln, ['oss'], ['orst'], scale=1.0 / 128, bias=EPS)
                    S.act(orst[:, 0:6], orst[:, 0:6], AF.Exp, ['orst'], ['orst'], scale=-0.5)
                    yield 3
                    S.tt(h3(qs[:, :], 6), h3(qs[:, :], 6), bc(orst[:, 0:6].unsqueeze(2), [128, 6, 128]), ALU.mult, ['qs', 'orst'], ['qs'])
                    S.tt(h3(qs[:, :], 6), h3(qs[:, :], 6), bc(gng[:, :].unsqueeze(1), [128, 6, 128]), ALU.mult, ['qs'] + C, ['qs'])
                    yield 3
                    S.tt(qs[:, :], qs[:, :], Pm[:, 792:1560], ALU.mult, ['qs', Pk], ['qs'])
                    pt, pkt = psG(2)
                    for k in range(6):
                        S.tr(psum[:, pt + k // 4, (k % 4) * 128:(k % 4) * 128 + 128], qs[:, 128 * k:128 * k + 128], ident,
                             ['qs', 'cst'], [pkt[k // 4]])
                    ob = tl['ob1']
                    S.copy('act', ob[:, 0:4, :], psum[:, pt, :].rearrange("p (a c) -> p a c", a=4), [pkt[0]], ['ob1'])
                    S.copy('act', ob[:, 4:6, :], psum[:, pt + 1, 0:256].rearrange("p (a c) -> p a c", a=2), [pkt[1]], ['ob1'])
                    yield 3
                    S.dma(Y_dram[:, 12:18, 128 * c:128 * c + 128], ob[:, :, :], ['ob1'], [('Yc', c)], q='act')


                def cd_gen():
                    for c in range(NCH):
                        Ak, Pk, Fk = 'A0', 'P0', 'F0'
                        A = A_[0]; Pm = P_[0]; F = FM[0]
                        S.dma(A[:, :], act_tm[128 * c:128 * c + 128, :], [], [Ak])
                        S.dma(Pm[:, :], ptm[128 * c:128 * c + 128, :], [], [Pk])
                        S.dma(F[:, :, :], fm_dram[:, :, 128 * c:128 * c + 128], [], [Fk])
                        yield 1
                        yield from merge([ssd_gen(c, A, Pm, F, Ak, Pk, Fk), gdn_gen(c, A, Pm, F, Ak, Pk, Fk)])

                dvec = tile(st, [128, 6])
                are = tile(st, [128, 24]); aim = tile(st, [128, 24]); lst = tile(st, [128, 24])
                m_t = tile(st, [128, 24]); fr = tile(st, [128, 24]); cre = tile(st, [128, 24]); cim = tile(st, [128, 24])
                ncim = tile(st, [128, 24]); tA = tile(st, [128, 24]); tB = tile(st, [128, 24]); tC = tile(st, [128, 24])
                tD = tile(st, [128, 24]); tE = tile(st, [128, 24])
                c512 = tile(st, [128, 24]); s512 = tile(st, [128, 24]); ns512 = tile(st, [128, 24])
                BPre = tile(st, [128, 4, 128]); BPim = tile(st, [128, 4, 128])
                CIre = tile(st, [128, 4, 128]); CIim = tile(st, [128, 4, 128])
                bbre = tile(st, [128, 4, 128]); bbim = tile(st, [128, 4, 128]); bt1 = tile(st, [128, 128])
                LBre = tile(st, [128, 4, 128]); LBim = tile(st, [128, 4, 128])
                LCre = tile(st, [128, 4, 128], BF16); LCren = tile(st, [128, 4, 128], BF16); LCimn = tile(st, [128, 4, 128], BF16)
                csT = tile(st, [128, 4, 512]); snT = tile(st, [128, 4, 512])
                w1 = tile(st, [128, 512]); w2 = tile(st, [128, 512]); pre = tile(st, [128, 512]); pim = tile(st, [128, 512])
                wre = tile(st, [128, 512]); wim = tile(st, [128, 512])
                q1 = tile(st, [128, 512], BF16); q2 = tile(st, [128, 512], BF16); q3 = tile(st, [128, 512], BF16)
                q4 = tile(st, [128, 512], BF16); bre = tile(st, [128, 512]); bim = tile(st, [128, 512])
                car = tile(st, [128, 8]); tmpc = tile(st, [128, 2])
                ubuf = [tile(st, [128, 512]) for _ in range(2)]
                vst = [tile(st, [128, 512], BF16) for _ in range(2)]

                def s5_gen():
                    for gl in range(2):
                        S.dma(are[64 * gl:64 * gl + 64, :], s5_a_re[l].rearrange("(s g) n -> g n s", g=2)[gl], [], ['are'], slow=True)
                        S.dma(aim[64 * gl:64 * gl + 64, :], s5_a_im[l].rearrange("(s g) n -> g n s", g=2)[gl], [], ['aim'], slow=True)
                        S.dma(lst[64 * gl:64 * gl + 64, :], s5_log_step[l].rearrange("(s g) -> g s", g=2)[gl].partition_broadcast(64),
                              [], ['lst'], slow=True)
                    S.dma(dvec[:, :], s5_d[l].rearrange("(c p) -> p c", p=128), [], ['dvec'], slow=True)
                    P = ['s5p']
                    S.act(lst[:, :], lst[:, :], AF.Exp, ['lst'], P)
                    S.ts(are[:, :], are[:, :], -1e-4, ALU.min, ['are'] + P, P)
                    S.tt(tA[:, :], are[:, :], lst[:, :], ALU.mult, P, P)
                    S.act(m_t[:, :], tA[:, :], AF.Exp, P, P)
                    S.tt(tA[:, :], aim[:, :], lst[:, :], ALU.mult, ['aim'] + P, P)
                    S.ts(tA[:, :], tA[:, :], 1.0 / (2 * np.pi), ALU.mult, P, P)
                    S.ts(tB[:, :], tA[:, :], MAGIC, ALU.add, P, P)
                    S.ts(tB[:, :], tB[:, :], MAGIC, ALU.subtract, P, P)
                    S.tt(fr[:, :], tA[:, :], tB[:, :], ALU.subtract, P, P)
                    yield 3
                    S.act(tB[:, :], fr[:, :], AF.Sin, P, P, scale=TWO_PI)
                    S.ts(tC[:, :], fr[:, :], -1.0, ALU.mult, P, P)
                    S.tt(tC[:, :], tC[:, :], fr[:, :], ALU.max, P, P)
                    S.act(tC[:, :], tC[:, :], AF.Sin, P, P, scale=-TWO_PI, bias=float(np.pi / 2))
                    S.tt(tB[:, :], tB[:, :], m_t[:, :], ALU.mult, P, P)
                    S.tt(tC[:, :], tC[:, :], m_t[:, :], ALU.mult, P, P)
                    S.ts(tC[:, :], tC[:, :], -1.0, ALU.add, P, P)
                    S.tt(tD[:, :], are[:, :], are[:, :], ALU.mult, P, P)
                    S.tt(tE[:, :], aim[:, :], aim[:, :], ALU.mult, P, P)
                    S.tt(tD[:, :], tD[:, :], tE[:, :], ALU.add, P, P)
                    S.op('dve', lambda e: e.reciprocal(out=tD[:, :], in_=tD[:, :]), P, P)
                    yield 3
                    S.tt(tA[:, :], tC[:, :], are[:, :], ALU.mult, P, P)
                    S.tt(tE[:, :], tB[:, :], aim[:, :], ALU.mult, P, P)
                    S.tt(tA[:, :], tA[:, :], tE[:, :], ALU.add, P, P)
                    S.tt(cre[:, :], tA[:, :], tD[:, :], ALU.mult, P, P)
                    S.tt(tA[:, :], tB[:, :], are[:, :], ALU.mult, P, P)
                    S.tt(tE[:, :], tC[:, :], aim[:, :], ALU.mult, P, P)
                    S.tt(tA[:, :], tA[:, :], tE[:, :], ALU.subtract, P, P)
                    S.tt(cim[:, :], tA[:, :], tD[:, :], ALU.mult, P, P)
                    S.ts(ncim[:, :], cim[:, :], -1.0, ALU.mult, P, P)
                    S.ts(tA[:, :], fr[:, :], 512.0, ALU.mult, P, P)
                    S.ts(tB[:, :], tA[:, :], MAGIC, ALU.add, P, P)
                    S.ts(tB[:, :], tB[:, :], MAGIC, ALU.subtract, P, P)
                    S.tt(tA[:, :], tA[:, :], tB[:, :], ALU.subtract, P, P)
                    S.act(s512[:, :], tA[:, :], AF.Sin, P, P, scale=TWO_PI)
                    S.ts(ns512[:, :], s512[:, :], -1.0, ALU.mult, P, P)
                    S.ts(tB[:, :], tA[:, :], -1.0, ALU.mult, P, P)
                    S.tt(tB[:, :], tB[:, :], tA[:, :], ALU.max, P, P)
                    S.act(c512[:, :], tB[:, :], AF.Sin, P, P, scale=-TWO_PI, bias=float(np.pi / 2))
                    yield 3
                    for ct in range(6):
                        for tl_ in (BPre, BPim, CIre, CIim):
                            S.op('pool', lambda e, tl_=tl_: e.memset(tl_[:, :, :], 0.0), [], ['bprep'])
                        for gl in range(2):
                            for q in range(4):
                                g = 8 * ct + 2 * q + gl
                                c0 = (2 * q + gl) * 16
                                S.dma(BPre[64 * gl:64 * gl + 64, q, c0:c0 + 16], s5_b_re[l, g, :, :], [], ['bprep'])
                                S.dma(BPim[64 * gl:64 * gl + 64, q, c0:c0 + 16], s5_b_im[l, g, :, :], [], ['bprep'])
                                S.dma(CIre[c0:c0 + 16, q, 64 * gl:64 * gl + 64], s5_c_re[l, g, :, :], [], ['bprep'])
                                S.dma(CIim[c0:c0 + 16, q, 64 * gl:64 * gl + 64], s5_c_im[l, g, :, :], [], ['bprep'])
                            yield 1
                        for q in range(4):
                            s_ = 4 * ct + q
                            S.ts(bt1[:, :], BPre[:, q, :], cre[:, s_:s_ + 1], ALU.mult, ['bprep'] + P, ['bt1'])
                            S.stt(bbre[:, q, :], BPim[:, q, :], ncim[:, s_:s_ + 1], bt1[:, :], ALU.mult, ALU.add, ['bprep', 'bt1'] + P, ['bb'])
                            S.ts(bt1[:, :], BPim[:, q, :], cre[:, s_:s_ + 1], ALU.mult, ['bprep'] + P, ['bt1'])
                            S.stt(bbim[:, q, :], BPre[:, q, :], cim[:, s_:s_ + 1], bt1[:, :], ALU.mult, ALU.add, ['bprep', 'bt1'] + P, ['bb'])
                        yield 2
                        for src_, dst_, sc in ((bbre, LBre, 1.0), (bbim, LBim, 1.0), (CIre, LCre, 1.0), (CIre, LCren, -1.0),
                                               (CIim, LCimn, -1.0)):
                            pb = 1
                            pk = [('ps', 1)]
                            for q in range(4):
                                S.tr(psum[:, pb, q * 128:q * 128 + 128], src_[:, q, :], ident, ['bb', 'bprep', 'cst'], pk)
                            S.act(dst_[:, :, :], psum[:, pb, :].rearrange("p (a c) -> p a c", a=4), AF.Copy, pk, ['LBC'], scale=sc)
                            yield 1
                        for q in range(4):
                            s_ = 4 * ct + q
                            S.ts(snT[:, q, :], tidx[:, :], fr[:, s_:s_ + 1], ALU.mult, ['cst'] + P, ['snT'])
                            S.ts(pre[:, :], snT[:, q, :], MAGIC, ALU.add, ['snT'], ['pre'])
                            S.ts(pre[:, :], pre[:, :], MAGIC, ALU.subtract, ['pre'], ['pre'])
                            S.tt(snT[:, q, :], snT[:, q, :], pre[:, :], ALU.subtract, ['snT', 'pre'], ['snT'])
                            S.ts(csT[:, q, :], snT[:, q, :], -1.0, ALU.mult, ['snT'], ['csT'])
                            S.tt(csT[:, q, :], csT[:, q, :], snT[:, q, :], ALU.max, ['snT', 'csT'], ['csT'])
                            yield 3
                        S.act(snT[:, :, :], snT[:, :, :], AF.Sin, ['snT'], ['snT'], scale=TWO_PI)
                        S.act(csT[:, :, :], csT[:, :, :], AF.Sin, ['csT'], ['csT'], scale=-TWO_PI, bias=float(np.pi / 2))
                        yield 4
                        S.op('pool', lambda e: e.memset(car[:, :], 0.0), [], ['car'])
                        for bq, (t0, n) in enumerate(TBLK):
                            ub = ubuf[bq % 2]
                            uk = 'ub%d' % (bq % 2)
                            S.dma(ub[:, 0:n], uz_dram[:, ct, t0:t0 + n], [], [uk])
                            pby = 0
                            pky = [('ps', 0)]
                            for q in range(4):
                                s_ = 4 * ct + q
                                cs = csT[:, q, 0:n]
                                sn = snT[:, q, 0:n]
                                pka = [('ps', 1), ('ps', 2)]
                                S.mm(psum[:, 1, 0:n], [(LBre[:, q, :], ub[:, 0:n])], ['LBC', uk], [pka[0]])
                                S.mm(psum[:, 2, 0:n], [(LBim[:, q, :], ub[:, 0:n])], ['LBC', uk], [pka[1]])
                                yield 2
                                S.tt(w1[:, 0:n], cs, psum[:, 1, 0:n], ALU.mult, ['csT', pka[0]], ['w1'])
                                S.tt(w2[:, 0:n], sn, psum[:, 2, 0:n], ALU.mult, ['snT', pka[1]], ['w2'])
                                S.tt(bre[:, 0:n], cs, psum[:, 2, 0:n], ALU.mult, ['csT', pka[1]], ['bre'])
                                S.tt(bim[:, 0:n], sn, psum[:, 1, 0:n], ALU.mult, ['snT', pka[0]], ['bim'])
                                S.tt(pre[:, 0:n], w1[:, 0:n], w2[:, 0:n], ALU.add, ['w1', 'w2'], ['pre'], eng='pool')
                                S.tt(pim[:, 0:n], bre[:, 0:n], bim[:, 0:n], ALU.subtract, ['bre', 'bim'], ['pim'], eng='pool')
                                yield 3
                                S.op('dve', lambda e: e.tensor_tensor_scan(out=wre[:, 0:n], data0=bc(m_t[:, s_:s_ + 1], [128, n]),
                                                                            data1=pre[:, 0:n], initial=car[:, 2 * q:2 * q + 1],
                                                                            op0=ALU.mult, op1=ALU.add),
                                     ['pre', 'car'] + P, ['wre'])
                                S.op('dve', lambda e: e.tensor_tensor_scan(out=wim[:, 0:n], data0=bc(m_t[:, s_:s_ + 1], [128, n]),
                                                                            data1=pim[:, 0:n], initial=car[:, 2 * q + 1:2 * q + 2],
                                                                            op0=ALU.mult, op1=ALU.add),
                                     ['pim', 'car'] + P, ['wim'])
                                if bq < 8:
                                    S.ts(tmpc[:, 0:1], wre[:, n - 1:n], c512[:, s_:s_ + 1], ALU.mult, ['wre'] + P, ['tmpc'])
                                    S.stt(car[:, 2 * q:2 * q + 1], wim[:, n - 1:n], ns512[:, s_:s_ + 1], tmpc[:, 0:1], ALU.mult, ALU.add,
                                          ['wim', 'tmpc'] + P, ['car'])
                                    S.ts(tmpc[:, 1:2], wre[:, n - 1:n], s512[:, s_:s_ + 1], ALU.mult, ['wre'] + P, ['tmpc'])
                                    S.stt(car[:, 2 * q + 1:2 * q + 2], wim[:, n - 1:n], c512[:, s_:s_ + 1], tmpc[:, 1:2], ALU.mult,
                                          ALU.add, ['wim', 'tmpc'] + P, ['car'])
                                yield 3
                                S.tt(q1[:, 0:n], cs, wre[:, 0:n], ALU.mult, ['csT', 'wre'], ['q1'], eng='pool')
                                S.tt(q2[:, 0:n], sn, wim[:, 0:n], ALU.mult, ['snT', 'wim'], ['q2'], eng='pool')
                                S.tt(q3[:, 0:n], sn, wre[:, 0:n], ALU.mult, ['snT', 'wre'], ['q3'], eng='pool')
                                S.tt(q4[:, 0:n], cs, wim[:, 0:n], ALU.mult, ['csT', 'wim'], ['q4'])
                                yield 3
                                S._deps('pe', ['q1', 'q2', 'q3', 'q4', 'LBC'], pky)
                                prs = [(LCre[:, q, :], q1[:, 0:n]), (LCren[:, q, :], q2[:, 0:n]),
                                       (LCimn[:, q, :], q3[:, 0:n]), (LCimn[:, q, :], q4[:, 0:n])]
                                inst = None
                                for pi, (lt, rh) in enumerate(prs):
                                    inst = nc.tensor.matmul(psum[:, pby, 0:n], lhsT=lt, rhs=rh, start=(q == 0 and pi == 0),
                                                            stop=(q == 3 and pi == 3))
                                S.cnt['pe'] += 1
                                inst.then_inc(S.sems['pe'], 1)
                                S._record(('pe', S.cnt['pe'], 'pe'), ['q1', 'q2', 'q3', 'q4', 'LBC'], pky)
                                yield 1
                            vb = vst[bq % 2]
                            vk = 'vst%d' % (bq % 2)
                            S.stt(pre[:, 0:n], ub[:, 0:n], dvec[:, ct:ct + 1], psum[:, pby, 0:n], ALU.mult, ALU.add,
                                  [uk, 'dvec'] + pky, ['pre'])
                            S.act(w1[:, 0:n], pre[:, 0:n], AF.Square, ['pre'], ['w1'])
                            S.ts(w1[:, 0:n], w1[:, 0:n], 0.044715, ALU.mult, ['w1'], ['w1'], s2=1.0, op1=ALU.add)
                            S.tt(w1[:, 0:n], w1[:, 0:n], pre[:, 0:n], ALU.mult, ['w1', 'pre'], ['w1'])
                            S.act(w2[:, 0:n], w1[:, 0:n], AF.Exp, ['w1'], ['w2'], scale=-1.5957691216057308)
                            S.ts(w2[:, 0:n], w2[:, 0:n], 1.0, ALU.add, ['w2'], ['w2'])
                            S.op('dve', lambda e: e.reciprocal(out=w2[:, 0:n], in_=w2[:, 0:n]), ['w2'], ['w2'])
                            S.tt(vb[:, 0:n], pre[:, 0:n], w2[:, 0:n], ALU.mult, ['pre', 'w2'], [vk])
                            S.dma(v_dram[:, ct, t0:t0 + n], vb[:, 0:n], [vk], [('vd', ct, bq)], q='act')
                            yield 4


                for _ in merge([s5_gen(), cd_gen()]):
                    pass
            S.barrier()


            with contextlib.ExitStack() as st:
                wst = [tile(st, [128, 8, 512]) for _ in range(2)]
                wglu = tile(st, [128, 6, 768], BF16)
                bglu = tile(st, [128, 6])
                S.dma(bglu[:, :], s5_b_glu[l].rearrange("(c p) -> p c", p=128), [], ['bglu'], slow=True)
                kglu = load_cast(st, wglu, s5_w_glu[l], 6, 768, wst, 'wglu', 'g')
                vb_ = [tile(st, [128, 6, 512], BF16) for _ in range(2)]
                szb = [tile(st, [128, 6, 512]) for _ in range(2)]
                sg = [tile(st, [128, 512]) for _ in range(2)]
                ya = [tile(st, [128, 6, 512], BF16) for _ in range(2)]
                it = 0
                for bq, (t0, n) in enumerate(TBLK):
                    bb_ = bq % 2
                    S.dma(vb_[bb_][:, :, 0:n], v_dram[:, :, t0:t0 + n], [], ['vb%d' % bb_])
                    S.dma(szb[bb_][:, :, 0:n], uz_dram[:, 6:12, t0:t0 + n], [], ['szb%d' % bb_])
                    for cto in range(6):
                        b = it % 2
                        it += 1
                        pb, pk = psalloc(1)
                        S.mm(psum[:, pb, 0:n], [(wglu[:, k, 128 * cto:128 * cto + 128], vb_[bb_][:, k, 0:n]) for k in range(6)],
                             kglu + ['vb%d' % bb_], pk)
                        S.act(sg[b][:, 0:n], psum[:, pb, 0:n], AF.Sigmoid, pk + ['bglu'], ['sg%d' % b], bias=bglu[:, cto:cto + 1])
                        S.tt(sg[b][:, 0:n], sg[b][:, 0:n], vb_[bb_][:, cto, 0:n], ALU.mult, ['sg%d' % b, 'vb%d' % bb_], ['sg%d' % b])
                        S.tt(ya[bb_][:, cto, 0:n], sg[b][:, 0:n], szb[bb_][:, cto, 0:n], ALU.mult, ['sg%d' % b, 'szb%d' % bb_],
                             [('ya', bb_, cto)], eng='pool')
                    S.dma(Y_dram[:, 0:6, t0:t0 + n], ya[bb_][:, :, 0:n], [('ya', bb_, c_) for c_ in range(6)], [('Ya', bq)], q='act')
            S.barrier()

            with contextlib.ExitStack() as st:
                wgate = tile(st, [128, 8, 3072], BF16)
                wbr = tile(st, [128, 18, 1024], BF16)
                wo = tile(st, [128, 8, 1024], BF16)
                st_outer = st
                st = st_outer.enter_context(contextlib.ExitStack())
                wst = [tile(st, [128, 8, 512]) for _ in range(2)]
                kg = load_cast(st, wgate, w_in[l, :, 6680:9752], 8, 3072, wst, 'wgate', 'g')
                kb = []
                for nb in range(3):
                    kb += load_cast(st, wbr[:, 6 * nb:6 * nb + 6, :], w_branch[l, nb], 6, 1024, wst, 'wbr%d' % nb, 'b')
                ko = load_cast(st, wo, w_out[l], 8, 1024, wst, 'wo', 'o')
                S.barrier()
                st.close()
                st = st_outer
                kg = []; kb = []; ko = []
                bg = tile(st, [128, 24])
                S.dma(bg[:, :].rearrange("p (n c) -> p n c", n=3), b_gate[l].rearrange("n (c p) -> p n c", p=128), [], ['bg'], slow=True)
                gb = tile(st, [128, D]); bb = tile(st, [128, D])
                S.dma(gb[:, :], ln_g[l].partition_broadcast(128), [], ['lnp'])
                S.dma(bb[:, :], ln_b[l].partition_broadcast(128), [], ['lnp'])
                lnt = (tile(st, [128, 2, 6]), tile(st, [128, 2]), tile(st, [128, 1]))
                hTb = [tile(st, [128, 8, 512], BF16) for _ in range(2)]
                Yb = [tile(st, [128, 18, 512], BF16) for _ in range(1)]
                mg = tile(st, [128, 8, 512], BF16)
                sgt = [tile(st, [128, 512]) for _ in range(3)]
                macc = tile(st, [128, 512])
                hin = [tile(st, [128, D]) for _ in range(2)]
                hout = [tile(st, [128, D]) for _ in range(2)]
                htb = [tile(st, [128, 8, 128], BF16) for _ in range(2)]
                tcount = 0
                for bq, (t0, n) in enumerate(TBLK):
                    b = bq % 2
                    S.dma(hTb[b][:, :, 0:n], hT_dram[:, :, t0:t0 + n], [], ['hTb%d' % b])
                    S.dma(Yb[0][:, :, 0:n], Y_dram[:, :, t0:t0 + n], [], ['Yb0'])
                    for dtile in range(8):
                        for nb in range(3):
                            pb, pk = psalloc(2)
                            col = 1024 * nb + 128 * dtile
                            S.mm(psum[:, pb, 0:n], [(wgate[:, k, col:col + 128], hTb[b][:, k, 0:n]) for k in range(8)],
                                 kg + ['hTb%d' % b], [pk[0]])
                            S.act(sgt[nb][:, 0:n], psum[:, pb, 0:n], AF.Sigmoid, [pk[0], 'bg'], ['sgt%d' % nb],
                                  bias=bg[:, 8 * nb + dtile:8 * nb + dtile + 1])
                            S.mm(psum[:, pb + 1, 0:n],
                                 [(wbr[:, 6 * nb + k, 128 * dtile:128 * dtile + 128], Yb[0][:, 6 * nb + k, 0:n]) for k in range(6)],
                                 kb + ['Yb0'], [pk[1]])
                            if nb == 0:
                                S.tt(macc[:, 0:n], sgt[nb][:, 0:n], psum[:, pb + 1, 0:n], ALU.mult, ['sgt%d' % nb, pk[1]], ['macc'])
                            else:
                                S.tt(sgt[nb][:, 0:n], sgt[nb][:, 0:n], psum[:, pb + 1, 0:n], ALU.mult, ['sgt%d' % nb, pk[1]], ['sgt%d' % nb])
                                if nb == 1:
                                    S.tt(macc[:, 0:n], macc[:, 0:n], sgt[nb][:, 0:n], ALU.add, ['macc', 'sgt%d' % nb], ['macc'], eng='pool')
                                else:
                                    S.tt(mg[:, dtile, 0:n], macc[:, 0:n], sgt[nb][:, 0:n], ALU.add, ['macc', 'sgt%d' % nb], [('mg', dtile)],
                                         eng='pool')
                    mgk = [('mg', d_) for d_ in range(8)]
                    for ti in range(n // 128):
                        i = t0 // 128 + ti
                        hb = tcount % 2
                        tcount += 1
                        S.dma(hin[hb][:, :], h_dram[128 * i:128 * i + 128, :], [('h', i)], ['hin%d' % hb])
                        pb, pk = psalloc(2)
                        for half in range(2):
                            S.mm(psum[:, pb + half, :],
                                 [(mg[:, k, 128 * ti:128 * ti + 128], wo[:, k, 512 * half:512 * half + 512]) for k in range(8)],
                                 mgk + ko, [pk[half]])
                            S.stt(hin[hb][:, 512 * half:512 * half + 512], hin[hb][:, 512 * half:512 * half + 512], float(ALPHA),
                                  psum[:, pb + half, :], ALU.mult, ALU.add, ['hin%d' % hb, pk[half]], ['hin%d' % hb])
                        layernorm(lnt, hin[hb][:, :], hout[hb][:, :], gb[:, :], bb[:, :], ['hin%d' % hb], ['hout%d' % hb])
                        if last_layer:
                            lo = 128 * i - 16
                            if i == 0:
                                S.dma(y_out[0:112, :], hout[hb][16:128, :], ['hout%d' % hb], [('yo', i)], q='act')
                            elif i < 32:
                                S.dma(y_out[lo:lo + 128, :], hout[hb][:, :], ['hout%d' % hb], [('yo', i)], q='act')
                            else:
                                S.dma(y_out[lo:lo + 16, :], hout[hb][0:16, :], ['hout%d' % hb], [('yo', i)], q='act')
                        else:
                            S.dma(h_dram[128 * i:128 * i + 128, :], hout[hb][:, :], ['hout%d' % hb], [('h', i)], q='act')
                            pt, pkt = psalloc(2)
                            for k in range(8):
                                S.tr(psum[:, pt + k // 4, (k % 4) * 128:(k % 4) * 128 + 128], hout[hb][:, 128 * k:128 * k + 128], ident,
                                     ['hout%d' % hb, 'cst'], [pkt[k // 4]])
                            S.copy('act', htb[hb][:, 0:4, :], psum[:, pt, :].rearrange("p (a c) -> p a c", a=4), [pkt[0]], ['htb%d' % hb])
                            S.copy('dve', htb[hb][:, 4:8, :], psum[:, pt + 1, :].rearrange("p (a c) -> p a c", a=4), [pkt[1]],
                                   ['htb%db' % hb])
                            S.dma(hT_dram[:, :, 128 * i:128 * i + 128], htb[hb][:, :, :], ['htb%d' % hb, 'htb%db' % hb], [('hT', i)], q='act')
            S.barrier()
            if debug and l == nlayers - 1:
                S.dma(dbg['act_tm'][:, :], act_tm[:, :], [], ['dbg1'])
                S.dma(dbg['ptm'][:, :], ptm[:, :], [], ['dbg2'])
                S.dma(dbg['Y'][:, :, :], Y_dram[:, :, :], [], ['dbg3'])
                S.dma(dbg['h'][:, :], h_dram[:, :], [], ['dbg4'])
                S.barrier()
        S.finish()
    return nc


PARAM_NAMES = ['meta', 'ln_in_g', 'ln_in_b', 'w_in', 's5_a_re', 's5_a_im', 's5_log_step', 's5_b_re', 's5_b_im',
               's5_c_re', 's5_c_im', 's5_d', 's5_w_glu', 's5_b_glu', 'ssd_conv_w', 'ssd_conv_b', 'ssd_dt_bias',
               'ssd_a_log', 'ssd_d', 'ssd_norm_g', 'gdn_conv_w', 'gdn_dt_bias', 'gdn_a_log', 'gdn_norm_g',
               'w_branch', 'b_gate', 'w_out', 'ln_g', 'ln_b']


def kernel(**inputs):
    x = np.ascontiguousarray(np.asarray(inputs['x'], dtype=np.float32))
    nb = x.shape[0]
    params = {k: np.ascontiguousarray(np.asarray(inputs[k], dtype=np.float32)) for k in PARAM_NAMES}
    consts = make_consts()
    nc = build()
    in_maps = []
    for b in range(nb):
        m = dict(params)
        m['x'] = x[b]
        m['consts'] = consts
        in_maps.append(m)
    res = run_bass_kernel_spmd(nc, in_maps, core_ids=list(range(nb)))
    out = np.stack([np.asarray(res.results[b]['y'], dtype=np.float32) for b in range(nb)], axis=0)
    return out
```

```python
import contextlib
import numpy as np
import concourse.bass as bass
import concourse.mybir as mybir
from concourse.bass_utils import run_bass_kernel_spmd

F32 = mybir.dt.float32
BF16 = mybir.dt.bfloat16
AF = mybir.ActivationFunctionType
ALU = mybir.AluOpType
AX = mybir.AxisListType

D = 1024
SEQ = 4096
NMETA = 16
T = 4112
TP = 4224
NCH = 33
DEPTH = 4
INW = 9752
ALPHA = 8 ** 0.25
EPS = 1e-5
NEG = -30000.0
MAGIC = 12582912.0
TWO_PI = 6.283185
TBLK = [(i * 512, 512) for i in range(8)] + [(4096, 128)]
NCONST = 2050


def make_consts():
    c = np.zeros((128, NCONST), np.float32)
    j = np.arange(128)[:, None]
    l = np.arange(128)[None, :]
    c[:, 0:128] = (j == l)
    c[:, 128:256] = (j <= l)
    c[:, 256:384] = (j > l)
    c[:, 384:512] = 1.0
    strict = np.where(l <= j, NEG, 0.0)
    incl = np.where(l < j, NEG, 0.0)
    c[:, 512:1024] = np.concatenate([strict, incl, strict, incl], axis=1)
    c[:, 1024:1536] = np.concatenate([incl] * 4, axis=1)
    c[:, 1536:2048] = np.arange(512)[None, :]
    c[:, 2048] = (np.arange(128) < 16)
    c[:, 2049] = np.where(np.arange(128) < 16, 0.0, 30000.0)
    return c


QMAP = {}


class Sched:
    def __init__(self, nc, es):
        self.nc = nc
        self.engs = {'pe': nc.tensor, 'act': nc.scalar, 'dve': nc.vector, 'pool': nc.gpsimd, 'sp': nc.sync}
        self.sems = {}
        for e in ['pe', 'act', 'dve', 'pool']:
            self.sems[e] = es.enter_context(nc.semaphore("s_" + e))
        self.cnt = {e: 0 for e in ['pe', 'act', 'dve', 'pool']}
        self.R = {'sp': 8, 'act': 4, 'pool': 4}
        self.dval = {}
        self.dn = {'sp': 0, 'act': 0, 'pool': 0}
        for qn, r in self.R.items():
            for i in range(r):
                self.sems[('d', qn, i)] = es.enter_context(nc.semaphore("sd%s%d" % (qn, i)))
                self.dval[(qn, i)] = 0
        self.waited = {}
        self.lastw = {}
        self.readers = {}
        self.latest = {}

    def _wait(self, eng, ev):
        semid, val, src = ev
        if eng == 'pe' and src == 'pe':
            return
        k = (eng, semid)
        if self.waited.get(k, 0) >= val:
            return
        self.waited[k] = val
        self.engs[eng].wait_ge(self.sems[semid], val)

    def _deps(self, eng, reads, writes):
        for k in reads:
            ev = self.lastw.get(k)
            if ev is not None:
                self._wait(eng, ev)
        for k in writes:
            ev = self.lastw.get(k)
            if ev is not None:
                self._wait(eng, ev)
            for ev in self.readers.get(k, {}).values():
                self._wait(eng, ev)

    def _record(self, ev, reads, writes):
        for k in reads:
            self.readers.setdefault(k, {})[ev[0]] = ev
        for k in writes:
            self.lastw[k] = ev
            self.readers[k] = {}
        self.latest[ev[0]] = ev

    def op(self, eng, fn, reads=(), writes=()):
        self._deps(eng, reads, writes)
        inst = fn(self.engs[eng])
        self.cnt[eng] += 1
        inst.then_inc(self.sems[eng], 1)
        self._record((eng, self.cnt[eng], eng), reads, writes)

    def mm(self, out, pairs, reads=(), writes=()):
        self._deps('pe', reads, writes)
        n = len(pairs)
        inst = None
        for i, (lhsT, rhs) in enumerate(pairs):
            inst = self.nc.tensor.matmul(out, lhsT=lhsT, rhs=rhs, start=(i == 0), stop=(i == n - 1))
        self.cnt['pe'] += 1
        inst.then_inc(self.sems['pe'], 1)
        self._record(('pe', self.cnt['pe'], 'pe'), reads, writes)

    def tr(self, out, in_, ident, reads=(), writes=()):
        self.op('pe', lambda e: e.transpose(out, in_, ident), reads, writes)

    def dma(self, out, in_, reads=(), writes=(), slow=False, q='sp'):
        q = QMAP.get(q, q)
        i = self.dn[q] % self.R[q]
        self.dn[q] += 1
        sid = ('d', q, i)
        if self.dval[(q, i)] > 0:
            self._wait(q, (sid, self.dval[(q, i)], q))
        self._deps(q, reads, writes)
        if slow:
            inst = self.engs[q].dma_start(out=out, in_=in_, allow_slow_non_contiguous=True)
        else:
            inst = self.engs[q].dma_start(out=out, in_=in_)
        self.dval[(q, i)] += 16
        inst.then_inc(self.sems[sid], 16)
        self._record((sid, self.dval[(q, i)], q), reads, writes)

    def barrier(self):
        evs = list(self.latest.values())
        for e in ['pe', 'act', 'dve', 'pool', 'sp']:
            for ev in evs:
                if ev[2] == e and e == 'pe':
                    continue
                self._wait(e, ev)
        self.lastw = {}
        self.readers = {}

    def finish(self):
        for ev in list(self.latest.values()):
            self._wait('sp', ev)

    def act(self, out, in_, func, reads, writes, bias=None, scale=1.0, accum_out=None):
        kw = {}
        if bias is not None:
            kw['bias'] = bias
        if accum_out is not None:
            kw['accum_out'] = accum_out
        self.op('act', lambda e: e.activation(out=out, in_=in_, func=func, scale=scale, **kw), reads, writes)

    def tt(self, out, a, b, op, reads, writes, eng='dve'):
        self.op(eng, lambda e: e.tensor_tensor(out=out, in0=a, in1=b, op=op), reads, writes)

    def ts(self, out, a, s1, op0, reads, writes, s2=None, op1=None, eng='dve'):
        if op1 is None:
            self.op(eng, lambda e: e.tensor_scalar(out=out, in0=a, scalar1=s1, scalar2=None, op0=op0), reads, writes)
        else:
            self.op(eng, lambda e: e.tensor_scalar(out=out, in0=a, scalar1=s1, scalar2=s2, op0=op0, op1=op1), reads, writes)

    def stt(self, out, in0, scalar, in1, op0, op1, reads, writes):
        self.op('dve', lambda e: e.scalar_tensor_tensor(out=out, in0=in0, scalar=scalar, in1=in1, op0=op0, op1=op1), reads, writes)

    def copy(self, eng, out, in_, reads, writes):
        if eng == 'act':
            self.act(out, in_, AF.Copy, reads, writes)
        else:
            self.op(eng, lambda e: e.tensor_copy(out=out, in_=in_), reads, writes)


def bc(ap, shape):
    return ap.broadcast_to(list(shape))


def build(nlayers=DEPTH, debug=False):
    nc = bass.Bass("TRN2", target_bir_lowering=False)

    def din(name, shape):
        return nc.dram_tensor(name, list(shape), F32, kind="ExternalInput").ap()

    x_in = din("x", [SEQ, D])
    meta = din("meta", [NMETA, D])
    ln_in_g = din("ln_in_g", [D])
    ln_in_b = din("ln_in_b", [D])
    w_in = din("w_in", [DEPTH, D, INW])
    s5_a_re = din("s5_a_re", [DEPTH, 48, 64])
    s5_a_im = din("s5_a_im", [DEPTH, 48, 64])
    s5_log_step = din("s5_log_step", [DEPTH, 48])
    s5_b_re = din("s5_b_re", [DEPTH, 48, 64, 16])
    s5_b_im = din("s5_b_im", [DEPTH, 48, 64, 16])
    s5_c_re = din("s5_c_re", [DEPTH, 48, 16, 64])
    s5_c_im = din("s5_c_im", [DEPTH, 48, 16, 64])
    s5_d = din("s5_d", [DEPTH, 768])
    s5_w_glu = din("s5_w_glu", [DEPTH, 768, 768])
    s5_b_glu = din("s5_b_glu", [DEPTH, 768])
    ssd_conv_w = din("ssd_conv_w", [DEPTH, 4, 1280])
    ssd_conv_b = din("ssd_conv_b", [DEPTH, 1280])
    ssd_dt_bias = din("ssd_dt_bias", [DEPTH, 12])
    ssd_a_log = din("ssd_a_log", [DEPTH, 12])
    ssd_d = din("ssd_d", [DEPTH, 12])
    ssd_norm_g = din("ssd_norm_g", [DEPTH, 768])
    gdn_conv_w = din("gdn_conv_w", [DEPTH, 4, 2304])
    gdn_dt_bias = din("gdn_dt_bias", [DEPTH, 6])
    gdn_a_log = din("gdn_a_log", [DEPTH, 6])
    gdn_norm_g = din("gdn_norm_g", [DEPTH, 128])
    w_branch = din("w_branch", [DEPTH, 3, 768, D])
    b_gate = din("b_gate", [DEPTH, 3, D])
    w_out = din("w_out", [DEPTH, D, D])
    ln_g = din("ln_g", [DEPTH, D])
    ln_b = din("ln_b", [DEPTH, D])
    consts_d = din("consts", [128, NCONST])
    y_out = nc.dram_tensor("y", [SEQ, D], F32, kind="ExternalOutput").ap()

    h_dram = nc.dram_tensor("h_dram", [TP, D], F32, kind="Internal").ap()
    hT_dram = nc.dram_tensor("hT_dram", [128, 8, TP], BF16, kind="Internal").ap()
    act_tm = nc.dram_tensor("act_tm", [TP, 3584], F32, kind="Internal").ap()
    ptm = nc.dram_tensor("ptm", [TP, 1560], F32, kind="Internal").ap()
    fm_dram = nc.dram_tensor("fm_dram", [128, 16, TP], F32, kind="Internal").ap()
    Y_dram = nc.dram_tensor("Y_dram", [128, 18, TP], BF16, kind="Internal").ap()
    uz_dram = nc.dram_tensor("uz_dram", [128, 12, TP], F32, kind="Internal").ap()
    v_dram = nc.dram_tensor("v_dram", [128, 6, TP], BF16, kind="Internal").ap()
    dbg = {}
    if debug:
        dbg['act_tm'] = nc.dram_tensor("dbg_act_tm", [TP, 3584], F32, kind="ExternalOutput").ap()
        dbg['ptm'] = nc.dram_tensor("dbg_ptm", [TP, 1560], F32, kind="ExternalOutput").ap()
        dbg['Y'] = nc.dram_tensor("dbg_Y", [128, 18, TP], BF16, kind="ExternalOutput").ap()
        dbg['h'] = nc.dram_tensor("dbg_h", [TP, D], F32, kind="ExternalOutput").ap()

    with contextlib.ExitStack() as es:
        S = Sched(nc, es)
        cst = es.enter_context(nc.sbuf_tensor("cst", [128, NCONST], F32))
        psum = es.enter_context(nc.psum_tensor("psum", [128, 8, 512], F32))
        S.dma(cst[:, :], consts_d[:, :], [], ['cst'])
        ident = cst[:, 0:128]
        triu = cst[:, 128:256]
        su = cst[:, 256:384]
        ones = cst[:, 384:512]
        mgdn = cst[:, 512:1024]
        mssd = cst[:, 1024:1536]
        tidx = cst[:, 1536:2048]
        padmask = cst[:, 2048:2049]
        padbig = cst[:, 2049:2050]

        psp = [0]

        pslo = [0]

        def psalloc(n):
            if psp[0] + n > 8 or psp[0] < pslo[0]:
                psp[0] = pslo[0]
            b = psp[0]
            psp[0] += n
            return b, [('ps', b + i) for i in range(n)]

        def mkps0(lo, hi):
            p = [lo]

            def alloc(n):
                if p[0] + n > hi:
                    p[0] = lo
                b_ = p[0]
                p[0] += n
                return b_, [('ps', b_ + i) for i in range(n)]
            return alloc

        def merge(gens):
            gens = list(gens)
            acc = [0.0] * len(gens)
            alive = list(range(len(gens)))
            while alive:
                i = min(alive, key=lambda a: acc[a])
                try:
                    c = next(gens[i])
                    acc[i] += c
                    yield c
                except StopIteration:
                    alive.remove(i)

        uid = [0]

        def tile(st, shape, dt=F32, name=None):
            uid[0] += 1
            nm = (name or "t") + "_%d" % uid[0]
            return st.enter_context(nc.sbuf_tensor(nm, list(shape), dt))

        def layernorm(st_tiles, src, dst, g_bc, b_bc, rk, wk):
            stats, mv, rstd = st_tiles
            S.op('dve', lambda e: e.bn_stats(out=stats[:, 0, :], in_=src[:, 0:512]), rk, ['ln_stats'])
            S.op('dve', lambda e: e.bn_stats(out=stats[:, 1, :], in_=src[:, 512:1024]), rk, ['ln_stats2'])
            S.op('dve', lambda e: e.bn_aggr(out=mv[:, :], in_=stats[:, :, :]), ['ln_stats', 'ln_stats2'], ['ln_mv'])
            S.act(rstd[:, :], mv[:, 1:2], AF.Sqrt, ['ln_mv'], ['ln_rstd'], bias=EPS)
            S.op('dve', lambda e: e.reciprocal(out=rstd[:, :], in_=rstd[:, :]), ['ln_rstd'], ['ln_rstd'])
            S.ts(dst, src, mv[:, 0:1], ALU.subtract, rk + ['ln_mv', 'ln_rstd'], wk, s2=rstd[:, 0:1], op1=ALU.mult)
            S.tt(dst, dst, g_bc, ALU.mult, wk + ['lnp'], wk)
            S.tt(dst, dst, b_bc, ALU.add, wk + ['lnp'], wk)

        def load_cast(st, dst_bf, src_dram, nk, ncols, wst, key_dst, tag):
            c0 = 0
            i = 0
            while c0 < ncols:
                n = min(512, ncols - c0)
                ws = wst[i % 2]
                wkey = 'wst%d' % (i % 2)
                S.dma(ws[:, 0:nk, 0:n], src_dram[:, c0:c0 + n].rearrange("(k p) c -> p k c", p=128), [], [wkey])
                h = nk // 2
                S.copy('act', dst_bf[:, 0:h, c0:c0 + n], ws[:, 0:h, 0:n], [wkey], [key_dst + '_a%d' % i])
                S.copy('dve', dst_bf[:, h:nk, c0:c0 + n], ws[:, h:nk, 0:n], [wkey], [key_dst + '_b%d' % i])
                c0 += n
                i += 1
            return [key_dst + '_a%d' % j for j in range(i)] + [key_dst + '_b%d' % j for j in range(i)]

        with contextlib.ExitStack() as st:
            xin = [tile(st, [128, D]) for _ in range(2)]
            hts = [tile(st, [128, D]) for _ in range(2)]
            htb = [tile(st, [128, 8, 128], BF16) for _ in range(2)]
            gb = tile(st, [128, D])
            bb = tile(st, [128, D])
            lnt = (tile(st, [128, 2, 6]), tile(st, [128, 2]), tile(st, [128, 1]))
            S.dma(gb[:, :], ln_in_g.partition_broadcast(128), [], ['lnp'])
            S.dma(bb[:, :], ln_in_b.partition_broadcast(128), [], ['lnp'])
            for i in range(NCH):
                b = i % 2
                xk, hk, tk = 'xin%d' % b, 'hts%d' % b, 'htb%d' % b
                xt = xin[b]
                if i == 0:
                    S.dma(xt[0:16, :], meta[:, :], [], [xk])
                    S.dma(xt[16:128, :], x_in[0:112, :], [], [xk + 'b'])
                    rk = [xk, xk + 'b']
                elif i < 32:
                    S.dma(xt[:, :], x_in[128 * i - 16:128 * i + 112, :], [xk + 'b'], [xk])
                    rk = [xk]
                else:
                    S.op('pool', lambda e: e.memset(xt[:, :], 0.0), [], [xk])
                    S.dma(xt[0:16, :], x_in[4080:4096, :], [], [xk])
                    rk = [xk]
                layernorm(lnt, xt[:, :], hts[b][:, :], gb[:, :], bb[:, :], rk, [hk])
                S.dma(h_dram[128 * i:128 * i + 128, :], hts[b][:, :], [hk], [('h', i)], q='act')
                pb, pk = psalloc(2)
                for k in range(8):
                    S.tr(psum[:, pb + k // 4, (k % 4) * 128:(k % 4) * 128 + 128], hts[b][:, 128 * k:128 * k + 128], ident,
                         [hk, 'cst'], [pk[k // 4]])
                S.copy('act', htb[b][:, 0:4, :], psum[:, pb, :].rearrange("p (a c) -> p a c", a=4), [pk[0]], [tk])
                S.copy('dve', htb[b][:, 4:8, :], psum[:, pb + 1, :].rearrange("p (a c) -> p a c", a=4), [pk[1]], [tk + 'b'])
                S.dma(hT_dram[:, :, 128 * i:128 * i + 128], htb[b][:, :, :], [tk, tk + 'b'], [('hT', i)], q='act')
        S.barrier()

        for l in range(nlayers):
            last_layer = (l == nlayers - 1) and not debug
            with contextlib.ExitStack() as st:
                hT = tile(st, [128, 8, TP], BF16)
                for k in range(8):
                    S.dma(hT[:, k, :], hT_dram[:, k, :], [], ['hTk%d' % k])
                hTk = ['hTk%d' % k for k in range(8)]
                st_outer = st
                cw = tile(st_outer, [128, 28, 4])
                cb = tile(st_outer, [128, 28])
                st = st_outer.enter_context(contextlib.ExitStack())
                wst = [tile(st, [128, 8, 512]) for _ in range(2)]
                wb = [tile(st, [128, 8, 512], BF16) for _ in range(2)]
                ost = [tile(st, [128, 512]) for _ in range(4)]
                S.op('pool', lambda e: e.memset(cb[:, :], 0.0), [], ['cb'])
                for tap in range(4):
                    S.dma(cw[:, 0:10, tap], ssd_conv_w[l, tap, :].rearrange("(j p) -> p j", p=128), [], ['cw'], slow=True)
                    S.dma(cw[:, 10:28, tap], gdn_conv_w[l, tap, :].rearrange("(j p) -> p j", p=128), [], ['cw'], slow=True)
                S.dma(cb[:, 0:10], ssd_conv_b[l, :].rearrange("(j p) -> p j", p=128), [], ['cb'], slow=True)
                blocks = [(2816, 512, 0), (3328, 268, 512), (5900, 512, 780), (6412, 268, 1292)]
                on = 0
                for bi, (c0, n, p0) in enumerate(blocks):
                    ws = wst[bi % 2]
                    wbb = wb[bi % 2]
                    S.dma(ws[:, :, 0:n], w_in[l, :, c0:c0 + n].rearrange("(k p) c -> p k c", p=128), [], ['wst%d' % (bi % 2)])
                    S.copy('act', wbb[:, 0:4, 0:n], ws[:, 0:4, 0:n], ['wst%d' % (bi % 2)], ['wb%da' % (bi % 2)])
                    S.copy('dve', wbb[:, 4:8, 0:n], ws[:, 4:8, 0:n], ['wst%d' % (bi % 2)], ['wb%db' % (bi % 2)])
                    wk = ['wb%da' % (bi % 2), 'wb%db' % (bi % 2)]
                    for i in range(NCH):
                        pb, pk = psalloc(1)
                        S.mm(psum[:, pb, 0:n], [(hT[:, k, 128 * i:128 * i + 128], wbb[:, k, 0:n]) for k in range(8)],
                             hTk + wk, pk)
                        o = ost[on % 4]
                        ok = 'ost%d' % (on % 4)
                        zs = 12 if bi in (0, 2) else 0
                        S.act(o[:, 0:n], psum[:, pb, 0:n], AF.Silu, pk, [ok])
                        if zs:
                            S.copy('dve', o[:, 0:zs], psum[:, pb, 0:zs], pk + [ok], [ok])
                        S.dma(ptm[128 * i:128 * i + 128, p0:p0 + n], o[:, 0:n], [ok], [('ptm', i, bi)], q='act')
                        on += 1
                S.barrier()
                st.close()
                st = st_outer
                xf = [tile(st, [128, 3 + TP]) for _ in range(2)]
                acc = [tile(st, [128, TP]) for _ in range(2)]
                rsb = [tile(st, [128, TP]) for _ in range(2)]
                wst2 = [tile(st, [128, 8, 128]) for _ in range(2)]
                wb2 = [tile(st, [128, 8, 128], BF16) for _ in range(2)]
                tstb = [[tile(st, [128, 4, 128]) for _ in range(2)] for _ in range(2)]
                for b in range(2):
                    S.op('pool', lambda e: e.memset(xf[b][:, 0:3], 0.0), [], ['xfz%d' % b])
                tnc = [0, 0]
                psP = [mkps0(0, 4), mkps0(4, 8)]
                def stA_hdr(jj):
                    b = jj % 2
                    j = jj - 12
                    if jj < 12:
                        c0 = 128 * jj
                    else:
                        c0 = 1536 + 128 * j if j < 10 else 3596 + 128 * (j - 10)
                    return b, j, c0

                def projP(jj):
                    b, j, c0 = stA_hdr(jj)
                    S.dma(wst2[b][:, :, :], w_in[l, :, c0:c0 + 128].rearrange("(k p) c -> p k c", p=128), [], ['wst2%d' % b])
                    S.copy('act', wb2[b][:, :, :], wst2[b][:, :, :], ['wst2%d' % b], ['wb2%d' % b])
                    xk = ['xf%d_%d' % (b, q) for q in range(9)]
                    for q, (t0, n) in enumerate(TBLK):
                        pb, pk = psP[b](1)
                        S.mm(psum[:, pb, 0:n], [(wb2[b][:, k, :], hT[:, k, t0:t0 + n]) for k in range(8)],
                             hTk + ['wb2%d' % b], pk)
                        S.copy('act' if q % 2 == 0 else 'dve', xf[b][:, 3 + t0:3 + t0 + n], psum[:, pb, 0:n], pk, [xk[q]])
                        yield 2

                def restC(jj):
                    b, j, c0 = stA_hdr(jj)
                    xk = ['xf%d_%d' % (b, q) for q in range(9)]
                    if jj < 12:
                        if jj >= 6:
                            S.act(xf[b][:, 3:3 + TP], xf[b][:, 3:3 + TP], AF.Silu, xk, xk)
                        S.dma(uz_dram[:, jj, :], xf[b][:, 3:3 + TP], xk, [('uz', jj)], q='act')
                        yield 2
                        return
                    ak = 'acc%d' % b
                    a = acc[b]
                    S.act(a[:, :], xf[b][:, 3:3 + TP], AF.Identity, xk + ['cw'], [ak], scale=cw[:, j, 3:4])
                    yield 4
                    for tap in (2, 1, 0):
                        S.stt(a[:, :], xf[b][:, tap:tap + TP], cw[:, j, tap:tap + 1], a[:, :], ALU.mult, ALU.add,
                              xk + ['xfz%d' % b, ak, 'cw'], [ak])
                        yield 5
                    S.act(a[:, :], a[:, :], AF.Silu, [ak, 'cb'], [ak], bias=cb[:, j:j + 1])
                    yield 4
                    if 10 <= j < 22:
                        S.act(rsb[b][:, :], a[:, :], AF.Square, [ak], ['rs%d' % b])
                        yield 4
                        for q, (t0, n) in enumerate(TBLK):
                            pb, pk = psP[b](1)
                            S.mm(psum[:, pb, 0:n], [(ones, rsb[b][:, t0:t0 + n])], ['rs%d' % b, 'cst'], pk)
                            S.act(xf[b][:, 3 + t0:3 + t0 + n], psum[:, pb, 0:n], AF.Ln, pk, [xk[q]], bias=1e-6)
                            yield 1
                        S.act(rsb[b][:, :], xf[b][:, 3:3 + TP], AF.Exp, xk, ['rs%d' % b], scale=-0.5)
                        yield 4
                        if j < 16:
                            S.stt(a[:, :], a[:, :], 128.0 ** -0.5, rsb[b][:, :], ALU.mult, ALU.mult, [ak, 'rs%d' % b], [ak])
                        else:
                            S.tt(a[:, :], a[:, :], rsb[b][:, :], ALU.mult, [ak, 'rs%d' % b], [ak])
                    if 6 <= j < 10:
                        S.dma(fm_dram[:, j - 6, :], a[:, :], [ak], [('fm', j)], q='act')
                    elif 10 <= j < 22:
                        S.dma(fm_dram[:, 4 + (j - 10), :], a[:, :], [ak], [('fm', j)], q='act')
                    for i0 in range(0, NCH, 4):
                        ni = min(4, NCH - i0)
                        pb, pk = psP[b](1)
                        for ii in range(ni):
                            i = i0 + ii
                            S.tr(psum[:, pb, ii * 128:ii * 128 + 128], a[:, 128 * i:128 * i + 128], ident, [ak, 'cst'], pk)
                        tb_ = tstb[b][tnc[b] % 2]
                        tk = 'tst%d_%d' % (b, tnc[b] % 2)
                        S.copy('act' if tnc[b] % 2 == 0 else 'dve', tb_[:, 0:ni, :],
                               psum[:, pb, 0:ni * 128].rearrange("p (a c) -> p a c", a=ni), pk, [tk])
                        S.dma(act_tm[128 * i0:128 * (i0 + ni), 128 * j:128 * j + 128].rearrange("(a p) c -> p a c", p=128),
                              tb_[:, 0:ni, :], [tk], [('act', j, i0)], q='act')
                        tnc[b] += 1
                        yield 2

                def stA_gen(par):
                    for jj in range(par, 40, 2):
                        yield from projP(jj)
                        yield from restC(jj)

                for _ in merge([stA_gen(0), stA_gen(1)]):
                    pass
            S.barrier()

            with contextlib.ExitStack() as st:
                def mkps(lo, hi):
                    p = [lo]

                    def alloc(n):
                        if p[0] + n > hi:
                            p[0] = lo
                        b = p[0]
                        p[0] += n
                        return b, [('ps', b + i) for i in range(n)]
                    return alloc
                psG = mkps(3, 6)
                psS = mkps(6, 8)

                dtb = tile(st, [128, 12]); ssdA = tile(st, [128, 12]); ssdD = tile(st, [128, 12]); ngb = tile(st, [128, 768])
                gdtb = tile(st, [128, 6]); gA = tile(st, [128, 6]); gng = tile(st, [128, 128])
                ST = tile(st, [128, 12, 64]); GS = tile(st, [128, 6, 128])
                C = ['cdp']
                S.dma(dtb[:, :], ssd_dt_bias[l].partition_broadcast(128), [], C)
                S.dma(ssdA[:, :], ssd_a_log[l].partition_broadcast(128), [], C)
                S.dma(ssdD[:, :], ssd_d[l].partition_broadcast(128), [], C)
                S.dma(ngb[:, :], ssd_norm_g[l].partition_broadcast(128), [], C)
                S.dma(gdtb[:, :], gdn_dt_bias[l].partition_broadcast(128), [], C)
                S.dma(gA[:, :], gdn_a_log[l].partition_broadcast(128), [], C)
                S.dma(gng[:, :], gdn_norm_g[l].partition_broadcast(128), [], C)
                S.act(ssdA[:, :], ssdA[:, :], AF.Exp, C, C)
                S.ts(ssdA[:, :], ssdA[:, :], -1.0, ALU.mult, C, C)
                S.act(gA[:, :], gA[:, :], AF.Exp, C, C)
                S.ts(gA[:, :], gA[:, :], -1.0, ALU.mult, C, C)
                S.op('pool', lambda e: e.memset(ST[:, :, :], 0.0), [], ['ST'])
                S.op('pool', lambda e: e.memset(GS[:, :, :], 0.0), [], ['GS'])

                A_ = [tile(st, [128, 3584]) for _ in range(1)]
                P_ = [tile(st, [128, 1560]) for _ in range(1)]
                FM = [tile(st, [128, 16, 128]) for _ in range(1)]
                tl = {}

                def mk(name, shape, dt=F32):
                    tl[name] = tile(st, shape, dt, name)

                for nm in ['dt', 'da', 'acum', 'eac', 'cd', 'toe', 'ssq', 'rstd', 'bet', 'lnb', 'gg', 'gcum', 'egc', 'eend',
                           'glast', 'bge', 'oss', 'orst']:
                    mk(nm, [128, 12])
                for nm in ['SD', 'G', 'dec', 'Gall', 'E', 'AA']:
                    mk(nm, [128, 1536])
                for nm in ['xd', 'xdte', 'yo', 'xD', 'szs', 'Bm0', 'Bm1', 'BT0', 'BT1', 'Pinv', 'bv', 'wkk', 'nwT', 'vn', 'qs',
                           'kd', 'osq']:
                    mk(nm, [128, 768])
                mk('ob0', [128, 6, 128], BF16)
                mk('ob1', [128, 6, 128], BF16)

                def h3(ap, h):
                    return ap.rearrange("p (h d) -> p h d", h=h)


                def ssd_gen(c, A, Pm, F, Ak, Pk, Fk):

                    dt, da, acum, eac, cd, toe = tl['dt'], tl['da'], tl['acum'], tl['eac'], tl['cd'], tl['toe']
                    S.tt(dt[:, :], Pm[:, 0:12], dtb[:, :], ALU.add, [Pk] + C, ['dt'])
                    S.act(dt[:, :], dt[:, :], AF.Exp, ['dt'], ['dt'])
                    S.act(dt[:, :], dt[:, :], AF.Ln, ['dt'], ['dt'], bias=1.0)
                    yield 3
                    S.tt(da[:, :], dt[:, :], ssdA[:, :], ALU.mult, ['dt'] + C, ['da'])
                    pb, pk = psS(1)
                    S.mm(psum[:, pb, 0:12], [(triu, da[:, :])], ['da', 'cst'], pk)
                    S.mm(psum[:, pb, 16:28], [(ones, da[:, :])], ['da', 'cst'], pk)
                    yield 3
                    S.copy('dve', acum[:, :], psum[:, pb, 0:12], pk, ['acum'])
                    S.act(eac[:, :], psum[:, pb, 0:12], AF.Exp, pk, ['eac'])
                    S.act(cd[:, :], psum[:, pb, 16:28], AF.Exp, pk, ['cd'])
                    yield 3
                    S.tt(toe[:, :], psum[:, pb, 16:28], acum[:, :], ALU.subtract, pk + ['acum'], ['toe'])
                    S.act(toe[:, :], toe[:, :], AF.Exp, ['toe'], ['toe'])
                    xd, xdte, SD, yo, G, dec, xD, szs = tl['xd'], tl['xdte'], tl['SD'], tl['yo'], tl['G'], tl['dec'], tl['xD'], tl['szs']
                    Ax = h3(A[:, 0:768], 12)
                    S.tt(h3(xd[:, :], 12), Ax, bc(dt[:, :].unsqueeze(2), [128, 12, 64]), ALU.mult, [Ak, 'dt'], ['xd'])
                    yield 3
                    S.tt(h3(xdte[:, :], 12), h3(xd[:, :], 12), bc(toe[:, :].unsqueeze(2), [128, 12, 64]), ALU.mult, ['xd', 'toe'],
                         ['xdte'], eng='pool')
                    S.tt(h3(G[:, :], 12), bc(triu.unsqueeze(1), [128, 12, 128]), bc(da[:, :].unsqueeze(2), [128, 12, 128]), ALU.mult,
                         ['cst', 'da'], ['G'])
                    for q in range(3):
                        pd, pkd = psS(1)
                        S.mm(psum[:, pd, :], [(su, G[:, 512 * q:512 * q + 512]), (ident, mssd)], ['G', 'cst'], pkd)
                        S.act(dec[:, 512 * q:512 * q + 512], psum[:, pd, :], AF.Exp, pkd, ['dec'])
                    yield 3
                    pc, pkc = psS(1)
                    for g in range(2):
                        S.mm(psum[:, pc, 128 * g:128 * g + 128], [(F[:, g, :], F[:, 2 + g, :])], [Fk], pkc)
                    S.tt(SD[:, :].rearrange("p (g h d) -> p g h d", g=2, h=6),
                         dec[:, :].rearrange("p (g h d) -> p g h d", g=2, h=6),
                         bc(psum[:, pc, 0:256].rearrange("p (g d) -> p g d", g=2).unsqueeze(2), [128, 2, 6, 128]),
                         ALU.mult, ['dec'] + pkc, ['SD'])
                    po, pko = psS(2)
                    for g in range(2):
                        S.mm(psum[:, po + g, 0:384], [(F[:, 2 + g, :], ST[:, 6 * g:6 * g + 6, :])], [Fk, 'ST'], [pko[g]])
                    S.tt(yo[:, :].rearrange("p (g h d) -> p g h d", g=2, h=6),
                         psum[:, po:po + 2, 0:384].rearrange("p g (h d) -> p g h d", h=6),
                         bc(eac[:, :].rearrange("p (g h) -> p g h", g=2).unsqueeze(3), [128, 2, 6, 64]),
                         ALU.mult, pko + ['eac'], ['yo'])
                    py, pky = psS(2)
                    for h in range(12):
                        S.mm(psum[:, py + h // 8, (h % 8) * 64:(h % 8) * 64 + 64],
                             [(SD[:, 128 * h:128 * h + 128], xd[:, 64 * h:64 * h + 64])], ['SD', 'xd'], [pky[h // 8]])
                    S.tt(yo[:, 0:512], yo[:, 0:512], psum[:, py, :], ALU.add, ['yo', pky[0]], ['yo'])
                    yield 3
                    S.tt(yo[:, 512:768], yo[:, 512:768], psum[:, py + 1, 0:256], ALU.add, ['yo', pky[1]], ['yo'])
                    S.tt(h3(xD[:, :], 12), Ax, bc(ssdD[:, :].unsqueeze(2), [128, 12, 64]), ALU.mult, [Ak] + C, ['xD'], eng='pool')
                    S.tt(yo[:, :], yo[:, :], xD[:, :], ALU.add, ['yo', 'xD'], ['yo'])
                    yield 3
                    pn, pkn = psS(2)
                    for g in range(2):
                        S.mm(psum[:, pn + g, 0:384], [(A[:, 768 + 128 * g:768 + 128 * g + 128], xdte[:, 384 * g:384 * g + 384])],
                             [Ak, 'xdte'], [pkn[g]])
                    S.tt(ST[:, :, :], ST[:, :, :], bc(cd[:, :].unsqueeze(2), [128, 12, 64]), ALU.mult, ['ST', 'cd'], ['ST'])
                    S.tt(ST[:, :, :].rearrange("p (g h) d -> p g (h d)", g=2), ST[:, :, :].rearrange("p (g h) d -> p g (h d)", g=2),
                         psum[:, pn:pn + 2, 0:384], ALU.add, ['ST'] + pkn, ['ST'])
                    ssq, rstd = tl['ssq'], tl['rstd']
                    yield 3
                    S.tt(yo[:, :], yo[:, :], Pm[:, 12:780], ALU.mult, ['yo', Pk], ['yo'])
                    S.act(szs[:, :], yo[:, :], AF.Square, ['yo'], ['szs'])
                    S.op('dve', lambda e: e.tensor_reduce(out=ssq[:, 0:1], in_=szs[:, :], axis=AX.X, op=ALU.add), ['szs'], ['ssq'])
                    yield 3
                    S.act(rstd[:, 0:1], ssq[:, 0:1], AF.Ln, ['ssq'], ['rstd'], scale=1.0 / 768, bias=EPS)
                    S.act(rstd[:, 0:1], rstd[:, 0:1], AF.Exp, ['rstd'], ['rstd'], scale=-0.5)
                    S.stt(yo[:, :], yo[:, :], rstd[:, 0:1], ngb[:, :], ALU.mult, ALU.mult, ['yo', 'rstd'] + C, ['yo'])
                    yield 3
                    pt, pkt = psS(2)
                    for k in range(6):
                        S.tr(psum[:, pt + k // 4, (k % 4) * 128:(k % 4) * 128 + 128], yo[:, 128 * k:128 * k + 128], ident,
                             ['yo', 'cst'], [pkt[k // 4]])
                    ob = tl['ob0']
                    S.copy('act', ob[:, 0:4, :], psum[:, pt, :].rearrange("p (a c) -> p a c", a=4), [pkt[0]], ['ob0'])
                    S.copy('act', ob[:, 4:6, :], psum[:, pt + 1, 0:256].rearrange("p (a c) -> p a c", a=2), [pkt[1]], ['ob0'])
                    S.dma(Y_dram[:, 6:12, 128 * c:128 * c + 128], ob[:, :, :], ['ob0'], [('Yb', c)], q='act')
                    yield 3


                def gdn_gen(c, A, Pm, F, Ak, Pk, Fk):

                    bet, lnb, gg, gcum, egc, eend, glast, bge = (tl['bet'], tl['lnb'], tl['gg'], tl['gcum'], tl['egc'], tl['eend'],
                                                                 tl['glast'], tl['bge'])
                    S.act(bet[:, 0:6], Pm[:, 786:792], AF.Exp, [Pk], ['bet'], scale=-1.0)
                    S.act(lnb[:, 0:6], bet[:, 0:6], AF.Ln, ['bet'], ['lnb'], bias=1.0)
                    S.ts(bet[:, 0:6], bet[:, 0:6], 1.0, ALU.add, ['bet', 'lnb'], ['bet'])
                    S.op('dve', lambda e: e.reciprocal(out=bet[:, 0:6], in_=bet[:, 0:6]), ['bet'], ['bet'])
                    if c == NCH - 1:
                        S.ts(bet[:, 0:6], bet[:, 0:6], padmask, ALU.mult, ['bet', 'cst'], ['bet'])
                        S.ts(lnb[:, 0:6], lnb[:, 0:6], padmask, ALU.mult, ['lnb', 'cst'], ['lnb'], s2=padbig, op1=ALU.add)
                    S.tt(gg[:, 0:6], Pm[:, 780:786], gdtb[:, :], ALU.add, [Pk] + C, ['gg'])
                    yield 3
                    S.act(gg[:, 0:6], gg[:, 0:6], AF.Exp, ['gg'], ['gg'])
                    S.act(gg[:, 0:6], gg[:, 0:6], AF.Ln, ['gg'], ['gg'], bias=1.0)
                    S.tt(gg[:, 0:6], gg[:, 0:6], gA[:, :], ALU.mult, ['gg'] + C, ['gg'])
                    yield 3
                    pb, pk = psG(1)
                    S.mm(psum[:, pb, 0:6], [(triu, gg[:, 0:6])], ['gg', 'cst'], pk)
                    S.mm(psum[:, pb, 8:14], [(ones, gg[:, 0:6])], ['gg', 'cst'], pk)
                    S.copy('dve', gcum[:, 0:6], psum[:, pb, 0:6], pk, ['gcum'])
                    yield 3
                    S.act(egc[:, 0:6], psum[:, pb, 0:6], AF.Exp, pk, ['egc'])
                    S.act(glast[:, 0:6], psum[:, pb, 8:14], AF.Exp, pk, ['glast'])
                    S.tt(eend[:, 0:6], psum[:, pb, 8:14], gcum[:, 0:6], ALU.subtract, pk + ['gcum'], ['eend'])
                    yield 3
                    S.act(eend[:, 0:6], eend[:, 0:6], AF.Exp, ['eend'], ['eend'])
                    S.tt(bge[:, 0:6], bet[:, 0:6], egc[:, 0:6], ALU.mult, ['bet', 'egc'], ['bge'])
                    Gall, E, AA = tl['Gall'], tl['E'], tl['AA']
                    G4 = Gall[:, :].rearrange("p (h a d) -> p h a d", h=6, a=2)
                    S.tt(G4[:, :, 1, :], bc(triu.unsqueeze(1), [128, 6, 128]), bc(gg[:, 0:6].unsqueeze(2), [128, 6, 128]), ALU.mult,
                         ['cst', 'gg'], ['Gall1'])
                    yield 3
                    S.tt(G4[:, :, 0, :], bc(ident.unsqueeze(1), [128, 6, 128]), bc(lnb[:, 0:6].unsqueeze(2), [128, 6, 128]), ALU.mult,
                         ['cst', 'lnb'], ['Gall0'], eng='pool')
                    S.tt(G4[:, :, 0, :], G4[:, :, 1, :], G4[:, :, 0, :], ALU.subtract, ['Gall0', 'Gall1'], ['Gall0'])
                    pe_, pke = psG(3)
                    for q in range(3):
                        S.mm(psum[:, pe_ + q, :], [(su, Gall[:, 512 * q:512 * q + 512]), (ident, mgdn)], ['Gall0', 'Gall1', 'cst'],
                             [pke[q]])
                    S.act(E[:, :], psum[:, pe_:pe_ + 3, :].rearrange("p a c -> p (a c)"), AF.Exp, pke, ['E'])
                    yield 3
                    pkk, pkkk = psG(3)
                    for h in range(6):
                        bnk = pkk + h // 2
                        off = (h % 2) * 256
                        S.mm(psum[:, bnk, off:off + 128], [(F[:, 10 + h, :], F[:, 10 + h, :])], [Fk], [pkkk[h // 2]])
                        S.mm(psum[:, bnk, off + 128:off + 256], [(F[:, 10 + h, :], F[:, 4 + h, :])], [Fk], [pkkk[h // 2]])
                    S.tt(AA[:, :], E[:, :], psum[:, pkk:pkk + 3, :].rearrange("p a c -> p (a c)"), ALU.mult, ['E'] + pkkk, ['AA'])
                    AA4 = AA[:, :].rearrange("p (h a d) -> p h a d", h=6, a=2)
                    Bm = [tl['Bm0'], tl['Bm1']]
                    BT = [tl['BT0'], tl['BT1']]
                    Pv = tl['Pinv']
                    S.ts(h3(Bm[0][:, :], 6), AA4[:, :, 0, :], -1.0, ALU.mult, ['AA'], ['Bm0'])
                    pt, pkt = psG(2)
                    for h in range(6):
                        S.tr(psum[:, pt + h // 4, (h % 4) * 128:(h % 4) * 128 + 128], Bm[0][:, 128 * h:128 * h + 128], ident,
                             ['Bm0', 'cst'], [pkt[h // 4]])
                    S.copy('act', BT[0][:, 0:512], psum[:, pt, :], [pkt[0]], ['BT0'])
                    yield 3
                    S.copy('act', BT[0][:, 512:768], psum[:, pt + 1, 0:256], [pkt[1]], ['BT0'])
                    S.tt(h3(Pv[:, :], 6), h3(Bm[0][:, :], 6), bc(ident.unsqueeze(1), [128, 6, 128]), ALU.add, ['Bm0', 'cst'], ['Pinv'])
                    cur = 0
                    for lev in range(1, 7):
                        nxt = 1 - cur
                        bk, btk = 'Bm%d' % cur, 'BT%d' % cur
                        nbk, nbtk = 'Bm%d' % nxt, 'BT%d' % nxt
                        p2, pk2 = psG(2)
                        for h in range(6):
                            S.mm(psum[:, p2 + h // 4, (h % 4) * 128:(h % 4) * 128 + 128],
                                 [(Bm[cur][:, 128 * h:128 * h + 128], BT[cur][:, 128 * h:128 * h + 128])], [bk, btk], [pk2[h // 4]])
                        S.copy('act', BT[nxt][:, 0:512], psum[:, p2, :], [pk2[0]], [nbtk])
                        S.copy('act', BT[nxt][:, 512:768], psum[:, p2 + 1, 0:256], [pk2[1]], [nbtk])
                        if lev < 6:
                            p1 = 5
                            pk1 = [('ps', 5), ('ps', 5)]
                            for half, (h0, h1) in enumerate(((0, 4), (4, 6))):
                                for h in range(h0, h1):
                                    S.mm(psum[:, 5, (h - h0) * 128:(h - h0) * 128 + 128],
                                         [(BT[cur][:, 128 * h:128 * h + 128], Bm[cur][:, 128 * h:128 * h + 128])], [bk, btk],
                                         [('ps', 5)])
                                S.copy('act', Bm[nxt][:, 128 * h0:128 * h1], psum[:, 5, 0:128 * (h1 - h0)], [('ps', 5)], [nbk])
                        yield 2
                        p3, pk3 = psG(2)
                        for h in range(6):
                            S.mm(psum[:, p3 + h // 4, (h % 4) * 128:(h % 4) * 128 + 128],
                                 [(BT[nxt][:, 128 * h:128 * h + 128], Pv[:, 128 * h:128 * h + 128])], [nbtk, 'Pinv'], [pk3[h // 4]])
                        S.tt(Pv[:, 0:512], Pv[:, 0:512], psum[:, p3, :], ALU.add, ['Pinv', pk3[0]], ['Pinv'])
                        S.tt(Pv[:, 512:768], Pv[:, 512:768], psum[:, p3 + 1, 0:256], ALU.add, ['Pinv', pk3[1]], ['Pinv'])
                        yield 2
                        cur = nxt
                    bv, wkk, nwT, vn, qs, kd, osq = tl['bv'], tl['wkk'], tl['nwT'], tl['vn'], tl['qs'], tl['kd'], tl['osq']
                    Av = h3(A[:, 2816:3584], 6)
                    Akk = h3(A[:, 2048:2816], 6)
                    S.tt(h3(bv[:, :], 6), Av, bc(bet[:, 0:6].unsqueeze(2), [128, 6, 128]), ALU.mult, [Ak, 'bet'], ['bv'], eng='pool')
                    S.tt(h3(wkk[:, :], 6), Akk, bc(bge[:, 0:6].unsqueeze(2), [128, 6, 128]), ALU.mult, [Ak, 'bge'], ['wkk'], eng='pool')
                    S.tt(h3(kd[:, :], 6), Akk, bc(eend[:, 0:6].unsqueeze(2), [128, 6, 128]), ALU.mult, [Ak, 'eend'], ['kd'], eng='pool')
                    yield 3
                    pw, pkw = psG(2)
                    for h in range(6):
                        S.mm(psum[:, pw + h // 4, (h % 4) * 128:(h % 4) * 128 + 128],
                             [(wkk[:, 128 * h:128 * h + 128], Pv[:, 128 * h:128 * h + 128])], ['wkk', 'Pinv'], [pkw[h // 4]])
                    S.act(nwT[:, 0:512], psum[:, pw, :], AF.Copy, [pkw[0]], ['nwT'], scale=-1.0)
                    S.act(nwT[:, 512:768], psum[:, pw + 1, 0:256], AF.Copy, [pkw[1]], ['nwT'], scale=-1.0)
                    pv, pkv = psG(2)
                    for h in range(6):
                        S.mm(psum[:, pv + h // 4, (h % 4) * 128:(h % 4) * 128 + 128],
                             [(Pv[:, 128 * h:128 * h + 128], bv[:, 128 * h:128 * h + 128]),
                              (nwT[:, 128 * h:128 * h + 128], GS[:, h, :])], ['Pinv', 'bv', 'nwT', 'GS'], [pkv[h // 4]])
                    S.copy('act', vn[:, 0:512], psum[:, pv, :], [pkv[0]], ['vn'])
                    yield 3
                    S.copy('act', vn[:, 512:768], psum[:, pv + 1, 0:256], [pkv[1]], ['vn'])
                    pq, pkq = psG(2)
                    for h in range(6):
                        S.mm(psum[:, pq + h // 4, (h % 4) * 128:(h % 4) * 128 + 128], [(F[:, 4 + h, :], GS[:, h, :])], [Fk, 'GS'],
                             [pkq[h // 4]])
                    S.tt(h3(qs[:, 0:512], 4), h3(psum[:, pq, :], 4), bc(egc[:, 0:4].unsqueeze(2), [128, 4, 128]), ALU.mult,
                         [pkq[0], 'egc'], ['qs'])
                    S.tt(h3(qs[:, 512:768], 2), h3(psum[:, pq + 1, 0:256], 2), bc(egc[:, 4:6].unsqueeze(2), [128, 2, 128]), ALU.mult,
                         [pkq[1], 'egc'], ['qs'])
                    yield 3
                    pa, pka = psG(2)
                    for h in range(6):
                        S.mm(psum[:, pa + h // 4, (h % 4) * 128:(h % 4) * 128 + 128],
                             [(AA[:, 256 * h + 128:256 * h + 256], vn[:, 128 * h:128 * h + 128])], ['AA', 'vn'], [pka[h // 4]])
                    S.tt(qs[:, 0:512], qs[:, 0:512], psum[:, pa, :], ALU.add, ['qs', pka[0]], ['qs'])
                    S.tt(qs[:, 512:768], qs[:, 512:768], psum[:, pa + 1, 0:256], ALU.add, ['qs', pka[1]], ['qs'])
                    pu, pku = psG(2)
                    for h in range(6):
                        S.mm(psum[:, pu + h // 4, (h % 4) * 128:(h % 4) * 128 + 128],
                             [(kd[:, 128 * h:128 * h + 128], vn[:, 128 * h:128 * h + 128])], ['kd', 'vn'], [pku[h // 4]])
                    S.tt(GS[:, :, :], GS[:, :, :], bc(glast[:, 0:6].unsqueeze(2), [128, 6, 128]), ALU.mult, ['GS', 'glast'], ['GS'])
                    yield 3
                    S.tt(GS[:, 0:4, :].rearrange("p h d -> p (h d)"), GS[:, 0:4, :].rearrange("p h d -> p (h d)"), psum[:, pu, :],
                         ALU.add, ['GS', pku[0]], ['GS'])
                    S.tt(GS[:, 4:6, :].rearrange("p h d -> p (h d)"), GS[:, 4:6, :].rearrange("p h d -> p (h d)"),
                         psum[:, pu + 1, 0:256], ALU.add, ['GS', pku[1]], ['GS'])
                    oss, orst = tl['oss'], tl['orst']
                    S.act(osq[:, :], qs[:, :], AF.Square, ['qs'], ['osq'])
                    yield 3
                    S.op('dve', lambda e: e.tensor_reduce(out=oss[:, 0:6], in_=h3(osq[:, :], 6), axis=AX.X, op=ALU.add),
                         ['osq'], ['oss'])
                    S.act(orst[:, 0:6], oss[:, 0:6], AF.Ln, ['oss'], ['orst'], scale=1.0 / 128, bias=EPS)
                    S.act(orst[:, 0:6], orst[:, 0:6], AF.Exp, ['orst'], ['orst'], scale=-0.5)
                    yield 3
                    S.tt(h3(qs[:, :], 6), h3(qs[:, :], 6), bc(orst[:, 0:6].unsqueeze(2), [128, 6, 128]), ALU.mult, ['qs', 'orst'], ['qs'])
                    S.tt(h3(qs[:, :], 6), h3(qs[:, :], 6), bc(gng[:, :].unsqueeze(1), [128, 6, 128]), ALU.mult, ['qs'] + C, ['qs'])
                    yield 3
                    S.tt(qs[:, :], qs[:, :], Pm[:, 792:1560], ALU.mult, ['qs', Pk], ['qs'])
                    pt, pkt = psG(2)
                    for k in range(6):
                        S.tr(psum[:, pt + k // 4, (k % 4) * 128:(k % 4) * 128 + 128], qs[:, 128 * k:128 * k + 128], ident,
                             ['qs', 'cst'], [pkt[k // 4]])
                    ob = tl['ob1']
                    S.copy('act', ob[:, 0:4, :], psum[:, pt, :].rearrange("p (a c) -> p a c", a=4), [pkt[0]], ['ob1'])
                    S.copy('act', ob[:, 4:6, :], psum[:, pt + 1, 0:256].rearrange("p (a c) -> p a c", a=2), [pkt[1]], ['ob1'])
                    yield 3
                    S.dma(Y_dram[:, 12:18, 128 * c:128 * c + 128], ob[:, :, :], ['ob1'], [('Yc', c)], q='act')


                def cd_gen():
                    for c in range(NCH):
                        Ak, Pk, Fk = 'A0', 'P0', 'F0'
                        A = A_[0]; Pm = P_[0]; F = FM[0]
                        S.dma(A[:, :], act_tm[128 * c:128 * c + 128, :], [], [Ak])
                        S.dma(Pm[:, :], ptm[128 * c:128 * c + 128, :], [], [Pk])
                        S.dma(F[:, :, :], fm_dram[:, :, 128 * c:128 * c + 128], [], [Fk])
                        yield 1
                        yield from merge([ssd_gen(c, A, Pm, F, Ak, Pk, Fk), gdn_gen(c, A, Pm, F, Ak, Pk, Fk)])

                dvec = tile(st, [128, 6])
                are = tile(st, [128, 24]); aim = tile(st, [128, 24]); lst = tile(st, [128, 24])
                m_t = tile(st, [128, 24]); fr = tile(st, [128, 24]); cre = tile(st, [128, 24]); cim = tile(st, [128, 24])
                ncim = tile(st, [128, 24]); tA = tile(st, [128, 24]); tB = tile(st, [128, 24]); tC = tile(st, [128, 24])
                tD = tile(st, [128, 24]); tE = tile(st, [128, 24])
                c512 = tile(st, [128, 24]); s512 = tile(st, [128, 24]); ns512 = tile(st, [128, 24])
                BPre = tile(st, [128, 4, 128]); BPim = tile(st, [128, 4, 128])
                CIre = tile(st, [128, 4, 128]); CIim = tile(st, [128, 4, 128])
                bbre = tile(st, [128, 4, 128]); bbim = tile(st, [128, 4, 128]); bt1 = tile(st, [128, 128])
                LBre = tile(st, [128, 4, 128]); LBim = tile(st, [128, 4, 128])
                LCre = tile(st, [128, 4, 128], BF16); LCren = tile(st, [128, 4, 128], BF16); LCimn = tile(st, [128, 4, 128], BF16)
                csT = tile(st, [128, 4, 512]); snT = tile(st, [128, 4, 512])
                w1 = tile(st, [128, 512]); w2 = tile(st, [128, 512]); pre = tile(st, [128, 512]); pim = tile(st, [128, 512])
                wre = tile(st, [128, 512]); wim = tile(st, [128, 512])
                q1 = tile(st, [128, 512], BF16); q2 = tile(st, [128, 512], BF16); q3 = tile(st, [128, 512], BF16)
                q4 = tile(st, [128, 512], BF16); bre = tile(st, [128, 512]); bim = tile(st, [128, 512])
                car = tile(st, [128, 8]); tmpc = tile(st, [128, 2])
                ubuf = [tile(st, [128, 512]) for _ in range(2)]
                vst = [tile(st, [128, 512], BF16) for _ in range(2)]

                def s5_gen():
                    for gl in range(2):
                        S.dma(are[64 * gl:64 * gl + 64, :], s5_a_re[l].rearrange("(s g) n -> g n s", g=2)[gl], [], ['are'], slow=True)
                        S.dma(aim[64 * gl:64 * gl + 64, :], s5_a_im[l].rearrange("(s g) n -> g n s", g=2)[gl], [], ['aim'], slow=True)
                        S.dma(lst[64 * gl:64 * gl + 64, :], s5_log_step[l].rearrange("(s g) -> g s", g=2)[gl].partition_broadcast(64),
                              [], ['lst'], slow=True)
                    S.dma(dvec[:, :], s5_d[l].rearrange("(c p) -> p c", p=128), [], ['dvec'], slow=True)
                    P = ['s5p']
                    S.act(lst[:, :], lst[:, :], AF.Exp, ['lst'], P)
                    S.ts(are[:, :], are[:, :], -1e-4, ALU.min, ['are'] + P, P)
                    S.tt(tA[:, :], are[:, :], lst[:, :], ALU.mult, P, P)
                    S.act(m_t[:, :], tA[:, :], AF.Exp, P, P)
                    S.tt(tA[:, :], aim[:, :], lst[:, :], ALU.mult, ['aim'] + P, P)
                    S.ts(tA[:, :], tA[:, :], 1.0 / (2 * np.pi), ALU.mult, P, P)
                    S.ts(tB[:, :], tA[:, :], MAGIC, ALU.add, P, P)
                    S.ts(tB[:, :], tB[:, :], MAGIC, ALU.subtract, P, P)
                    S.tt(fr[:, :], tA[:, :], tB[:, :], ALU.subtract, P, P)
                    yield 3
                    S.act(tB[:, :], fr[:, :], AF.Sin, P, P, scale=TWO_PI)
                    S.ts(tC[:, :], fr[:, :], -1.0, ALU.mult, P, P)
                    S.tt(tC[:, :], tC[:, :], fr[:, :], ALU.max, P, P)
                    S.act(tC[:, :], tC[:, :], AF.Sin, P, P, scale=-TWO_PI, bias=float(np.pi / 2))
                    S.tt(tB[:, :], tB[:, :], m_t[:, :], ALU.mult, P, P)
                    S.tt(tC[:, :], tC[:, :], m_t[:, :], ALU.mult, P, P)
                    S.ts(tC[:, :], tC[:, :], -1.0, ALU.add, P, P)
                    S.tt(tD[:, :], are[:, :], are[:, :], ALU.mult, P, P)
                    S.tt(tE[:, :], aim[:, :], aim[:, :], ALU.mult, P, P)
                    S.tt(tD[:, :], tD[:, :], tE[:, :], ALU.add, P, P)
                    S.op('dve', lambda e: e.reciprocal(out=tD[:, :], in_=tD[:, :]), P, P)
                    yield 3
                    S.tt(tA[:, :], tC[:, :], are[:, :], ALU.mult, P, P)
                    S.tt(tE[:, :], tB[:, :], aim[:, :], ALU.mult, P, P)
                    S.tt(tA[:, :], tA[:, :], tE[:, :], ALU.add, P, P)
                    S.tt(cre[:, :], tA[:, :], tD[:, :], ALU.mult, P, P)
                    S.tt(tA[:, :], tB[:, :], are[:, :], ALU.mult, P, P)
                    S.tt(tE[:, :], tC[:, :], aim[:, :], ALU.mult, P, P)
                    S.tt(tA[:, :], tA[:, :], tE[:, :], ALU.subtract, P, P)
                    S.tt(cim[:, :], tA[:, :], tD[:, :], ALU.mult, P, P)
                    S.ts(ncim[:, :], cim[:, :], -1.0, ALU.mult, P, P)
                    S.ts(tA[:, :], fr[:, :], 512.0, ALU.mult, P, P)
                    S.ts(tB[:, :], tA[:, :], MAGIC, ALU.add, P, P)
                    S.ts(tB[:, :], tB[:, :], MAGIC, ALU.subtract, P, P)
                    S.tt(tA[:, :], tA[:, :], tB[:, :], ALU.subtract, P, P)
                    S.act(s512[:, :], tA[:, :], AF.Sin, P, P, scale=TWO_PI)
                    S.ts(ns512[:, :], s512[:, :], -1.0, ALU.mult, P, P)
                    S.ts(tB[:, :], tA[:, :], -1.0, ALU.mult, P, P)
                    S.tt(tB[:, :], tB[:, :], tA[:, :], ALU.max, P, P)
                    S.act(c512[:, :], tB[:, :], AF.Sin, P, P, scale=-TWO_PI, bias=float(np.pi / 2))
                    yield 3
                    for ct in range(6):
                        for tl_ in (BPre, BPim, CIre, CIim):
                            S.op('pool', lambda e, tl_=tl_: e.memset(tl_[:, :, :], 0.0), [], ['bprep'])
                        for gl in range(2):
                            for q in range(4):
                                g = 8 * ct + 2 * q + gl
                                c0 = (2 * q + gl) * 16
                                S.dma(BPre[64 * gl:64 * gl + 64, q, c0:c0 + 16], s5_b_re[l, g, :, :], [], ['bprep'])
                                S.dma(BPim[64 * gl:64 * gl + 64, q, c0:c0 + 16], s5_b_im[l, g, :, :], [], ['bprep'])
                                S.dma(CIre[c0:c0 + 16, q, 64 * gl:64 * gl + 64], s5_c_re[l, g, :, :], [], ['bprep'])
                                S.dma(CIim[c0:c0 + 16, q, 64 * gl:64 * gl + 64], s5_c_im[l, g, :, :], [], ['bprep'])
                            yield 1
                        for q in range(4):
                            s_ = 4 * ct + q
                            S.ts(bt1[:, :], BPre[:, q, :], cre[:, s_:s_ + 1], ALU.mult, ['bprep'] + P, ['bt1'])
                            S.stt(bbre[:, q, :], BPim[:, q, :], ncim[:, s_:s_ + 1], bt1[:, :], ALU.mult, ALU.add, ['bprep', 'bt1'] + P, ['bb'])
                            S.ts(bt1[:, :], BPim[:, q, :], cre[:, s_:s_ + 1], ALU.mult, ['bprep'] + P, ['bt1'])
                            S.stt(bbim[:, q, :], BPre[:, q, :], cim[:, s_:s_ + 1], bt1[:, :], ALU.mult, ALU.add, ['bprep', 'bt1'] + P, ['bb'])
                        yield 2
                        for src_, dst_, sc in ((bbre, LBre, 1.0), (bbim, LBim, 1.0), (CIre, LCre, 1.0), (CIre, LCren, -1.0),
                                               (CIim, LCimn, -1.0)):
                            pb = 1
                            pk = [('ps', 1)]
                            for q in range(4):
                                S.tr(psum[:, pb, q * 128:q * 128 + 128], src_[:, q, :], ident, ['bb', 'bprep', 'cst'], pk)
                            S.act(dst_[:, :, :], psum[:, pb, :].rearrange("p (a c) -> p a c", a=4), AF.Copy, pk, ['LBC'], scale=sc)
                            yield 1
                        for q in range(4):
                            s_ = 4 * ct + q
                            S.ts(snT[:, q, :], tidx[:, :], fr[:, s_:s_ + 1], ALU.mult, ['cst'] + P, ['snT'])
                            S.ts(pre[:, :], snT[:, q, :], MAGIC, ALU.add, ['snT'], ['pre'])
                            S.ts(pre[:, :], pre[:, :], MAGIC, ALU.subtract, ['pre'], ['pre'])
                            S.tt(snT[:, q, :], snT[:, q, :], pre[:, :], ALU.subtract, ['snT', 'pre'], ['snT'])
                            S.ts(csT[:, q, :], snT[:, q, :], -1.0, ALU.mult, ['snT'], ['csT'])
                            S.tt(csT[:, q, :], csT[:, q, :], snT[:, q, :], ALU.max, ['snT', 'csT'], ['csT'])
                            yield 3
                        S.act(snT[:, :, :], snT[:, :, :], AF.Sin, ['snT'], ['snT'], scale=TWO_PI)
                        S.act(csT[:, :, :], csT[:, :, :], AF.Sin, ['csT'], ['csT'], scale=-TWO_PI, bias=float(np.pi / 2))
                        yield 4
                        S.op('pool', lambda e: e.memset(car[:, :], 0.0), [], ['car'])
                        for bq, (t0, n) in enumerate(TBLK):
                            ub = ubuf[bq % 2]
                            uk = 'ub%d' % (bq % 2)
                            S.dma(ub[:, 0:n], uz_dram[:, ct, t0:t0 + n], [], [uk])
                            pby = 0
                            pky = [('ps', 0)]
                            for q in range(4):
                                s_ = 4 * ct + q
                                cs = csT[:, q, 0:n]
                                sn = snT[:, q, 0:n]
                                pka = [('ps', 1), ('ps', 2)]
                                S.mm(psum[:, 1, 0:n], [(LBre[:, q, :], ub[:, 0:n])], ['LBC', uk], [pka[0]])
                                S.mm(psum[:, 2, 0:n], [(LBim[:, q, :], ub[:, 0:n])], ['LBC', uk], [pka[1]])
                                yield 2
                                S.tt(w1[:, 0:n], cs, psum[:, 1, 0:n], ALU.mult, ['csT', pka[0]], ['w1'])
                                S.tt(w2[:, 0:n], sn, psum[:, 2, 0:n], ALU.mult, ['snT', pka[1]], ['w2'])
                                S.tt(bre[:, 0:n], cs, psum[:, 2, 0:n], ALU.mult, ['csT', pka[1]], ['bre'])
                                S.tt(bim[:, 0:n], sn, psum[:, 1, 0:n], ALU.mult, ['snT', pka[0]], ['bim'])
                                S.tt(pre[:, 0:n], w1[:, 0:n], w2[:, 0:n], ALU.add, ['w1', 'w2'], ['pre'], eng='pool')
                                S.tt(pim[:, 0:n], bre[:, 0:n], bim[:, 0:n], ALU.subtract, ['bre', 'bim'], ['pim'], eng='pool')
                                yield 3
                                S.op('dve', lambda e: e.tensor_tensor_scan(out=wre[:, 0:n], data0=bc(m_t[:, s_:s_ + 1], [128, n]),
                                                                            data1=pre[:, 0:n], initial=car[:, 2 * q:2 * q + 1],
                                                                            op0=ALU.mult, op1=ALU.add),
                                     ['pre', 'car'] + P, ['wre'])
                                S.op('dve', lambda e: e.tensor_tensor_scan(out=wim[:, 0:n], data0=bc(m_t[:, s_:s_ + 1], [128, n]),
                                                                            data1=pim[:, 0:n], initial=car[:, 2 * q + 1:2 * q + 2],
                                                                            op0=ALU.mult, op1=ALU.add),
                                     ['pim', 'car'] + P, ['wim'])
                                if bq < 8:
                                    S.ts(tmpc[:, 0:1], wre[:, n - 1:n], c512[:, s_:s_ + 1], ALU.mult, ['wre'] + P, ['tmpc'])
                                    S.stt(car[:, 2 * q:2 * q + 1], wim[:, n - 1:n], ns512[:, s_:s_ + 1], tmpc[:, 0:1], ALU.mult, ALU.add,
                                          ['wim', 'tmpc'] + P, ['car'])
                                    S.ts(tmpc[:, 1:2], wre[:, n - 1:n], s512[:, s_:s_ + 1], ALU.mult, ['wre'] + P, ['tmpc'])
                                    S.stt(car[:, 2 * q + 1:2 * q + 2], wim[:, n - 1:n], c512[:, s_:s_ + 1], tmpc[:, 1:2], ALU.mult,
                                          ALU.add, ['wim', 'tmpc'] + P, ['car'])
                                yield 3
                                S.tt(q1[:, 0:n], cs, wre[:, 0:n], ALU.mult, ['csT', 'wre'], ['q1'], eng='pool')
                                S.tt(q2[:, 0:n], sn, wim[:, 0:n], ALU.mult, ['snT', 'wim'], ['q2'], eng='pool')
                                S.tt(q3[:, 0:n], sn, wre[:, 0:n], ALU.mult, ['snT', 'wre'], ['q3'], eng='pool')
                                S.tt(q4[:, 0:n], cs, wim[:, 0:n], ALU.mult, ['csT', 'wim'], ['q4'], eng='pool')
                                yield 3
                                S._deps('pe', ['q1', 'q2', 'q3', 'q4', 'LBC'], pky)
                                prs = [(LCre[:, q, :], q1[:, 0:n]), (LCren[:, q, :], q2[:, 0:n]),
                                       (LCimn[:, q, :], q3[:, 0:n]), (LCimn[:, q, :], q4[:, 0:n])]
                                inst = None
                                for pi, (lt, rh) in enumerate(prs):
                                    inst = nc.tensor.matmul(psum[:, pby, 0:n], lhsT=lt, rhs=rh, start=(q == 0 and pi == 0),
                                                            stop=(q == 3 and pi == 3))
                                S.cnt['pe'] += 1
                                inst.then_inc(S.sems['pe'], 1)
                                S._record(('pe', S.cnt['pe'], 'pe'), ['q1', 'q2', 'q3', 'q4', 'LBC'], pky)
                                yield 1
                            vb = vst[bq % 2]
                            vk = 'vst%d' % (bq % 2)
                            S.stt(pre[:, 0:n], ub[:, 0:n], dvec[:, ct:ct + 1], psum[:, pby, 0:n], ALU.mult, ALU.add,
                                  [uk, 'dvec'] + pky, ['pre'])
                            S.act(w1[:, 0:n], pre[:, 0:n], AF.Square, ['pre'], ['w1'])
                            S.ts(w1[:, 0:n], w1[:, 0:n], 0.044715, ALU.mult, ['w1'], ['w1'], s2=1.0, op1=ALU.add)
                            S.tt(w1[:, 0:n], w1[:, 0:n], pre[:, 0:n], ALU.mult, ['w1', 'pre'], ['w1'])
                            S.act(w2[:, 0:n], w1[:, 0:n], AF.Exp, ['w1'], ['w2'], scale=-1.5957691216057308)
                            S.ts(w2[:, 0:n], w2[:, 0:n], 1e18, ALU.min, ['w2'], ['w2'])
                            S.act(w2[:, 0:n], w2[:, 0:n], AF.Ln, ['w2'], ['w2'], bias=1.0)
                            S.act(w2[:, 0:n], w2[:, 0:n], AF.Exp, ['w2'], ['w2'], scale=-1.0)
                            S.tt(vb[:, 0:n], pre[:, 0:n], w2[:, 0:n], ALU.mult, ['pre', 'w2'], [vk])
                            S.dma(v_dram[:, ct, t0:t0 + n], vb[:, 0:n], [vk], [('vd', ct, bq)], q='act')
                            yield 4


                for _ in merge([s5_gen(), cd_gen()]):
                    pass
            S.barrier()


            with contextlib.ExitStack() as st:
                wst = [tile(st, [128, 8, 512]) for _ in range(2)]
                wglu = tile(st, [128, 6, 768], BF16)
                bglu = tile(st, [128, 6])
                S.dma(bglu[:, :], s5_b_glu[l].rearrange("(c p) -> p c", p=128), [], ['bglu'], slow=True)
                kglu = load_cast(st, wglu, s5_w_glu[l], 6, 768, wst, 'wglu', 'g')
                vb_ = [tile(st, [128, 6, 512], BF16) for _ in range(2)]
                szb = [tile(st, [128, 6, 512]) for _ in range(2)]
                sg = [tile(st, [128, 512]) for _ in range(2)]
                ya = [tile(st, [128, 6, 512], BF16) for _ in range(2)]
                it = 0
                for bq, (t0, n) in enumerate(TBLK):
                    bb_ = bq % 2
                    S.dma(vb_[bb_][:, :, 0:n], v_dram[:, :, t0:t0 + n], [], ['vb%d' % bb_])
                    S.dma(szb[bb_][:, :, 0:n], uz_dram[:, 6:12, t0:t0 + n], [], ['szb%d' % bb_])
                    for cto in range(6):
                        b = it % 2
                        it += 1
                        pb, pk = psalloc(1)
                        S.mm(psum[:, pb, 0:n], [(wglu[:, k, 128 * cto:128 * cto + 128], vb_[bb_][:, k, 0:n]) for k in range(6)],
                             kglu + ['vb%d' % bb_], pk)
                        S.act(sg[b][:, 0:n], psum[:, pb, 0:n], AF.Sigmoid, pk + ['bglu'], ['sg%d' % b], bias=bglu[:, cto:cto + 1])
                        S.tt(sg[b][:, 0:n], sg[b][:, 0:n], vb_[bb_][:, cto, 0:n], ALU.mult, ['sg%d' % b, 'vb%d' % bb_], ['sg%d' % b])
                        S.tt(ya[bb_][:, cto, 0:n], sg[b][:, 0:n], szb[bb_][:, cto, 0:n], ALU.mult, ['sg%d' % b, 'szb%d' % bb_],
                             [('ya', bb_, cto)], eng='pool')
                    S.dma(Y_dram[:, 0:6, t0:t0 + n], ya[bb_][:, :, 0:n], [('ya', bb_, c_) for c_ in range(6)], [('Ya', bq)], q='act')
            S.barrier()

            with contextlib.ExitStack() as st:
                wgate = tile(st, [128, 8, 3072], BF16)
                wbr = tile(st, [128, 18, 1024], BF16)
                wo = tile(st, [128, 8, 1024], BF16)
                st_outer = st
                st = st_outer.enter_context(contextlib.ExitStack())
                wst = [tile(st, [128, 8, 512]) for _ in range(2)]
                kg = load_cast(st, wgate, w_in[l, :, 6680:9752], 8, 3072, wst, 'wgate', 'g')
                kb = []
                for nb in range(3):
                    kb += load_cast(st, wbr[:, 6 * nb:6 * nb + 6, :], w_branch[l, nb], 6, 1024, wst, 'wbr%d' % nb, 'b')
                ko = load_cast(st, wo, w_out[l], 8, 1024, wst, 'wo', 'o')
                S.barrier()
                st.close()
                st = st_outer
                kg = []; kb = []; ko = []
                bg = tile(st, [128, 24])
                S.dma(bg[:, :].rearrange("p (n c) -> p n c", n=3), b_gate[l].rearrange("n (c p) -> p n c", p=128), [], ['bg'], slow=True)
                gb = tile(st, [128, D]); bb = tile(st, [128, D])
                S.dma(gb[:, :], ln_g[l].partition_broadcast(128), [], ['lnp'])
                S.dma(bb[:, :], ln_b[l].partition_broadcast(128), [], ['lnp'])
                lnt = (tile(st, [128, 2, 6]), tile(st, [128, 2]), tile(st, [128, 1]))
                hTb = [tile(st, [128, 8, 512], BF16) for _ in range(2)]
                Yb = [tile(st, [128, 18, 512], BF16) for _ in range(1)]
                mg = tile(st, [128, 8, 512], BF16)
                sgt = [tile(st, [128, 512]) for _ in range(3)]
                macc = tile(st, [128, 512])
                hin = [tile(st, [128, D]) for _ in range(2)]
                hout = [tile(st, [128, D]) for _ in range(2)]
                htb = [tile(st, [128, 8, 128], BF16) for _ in range(2)]
                tcount = 0
                for bq, (t0, n) in enumerate(TBLK):
                    b = bq % 2
                    S.dma(hTb[b][:, :, 0:n], hT_dram[:, :, t0:t0 + n], [], ['hTb%d' % b])
                    S.dma(Yb[0][:, :, 0:n], Y_dram[:, :, t0:t0 + n], [], ['Yb0'])
                    for dtile in range(8):
                        for nb in range(3):
                            pb, pk = psalloc(2)
                            col = 1024 * nb + 128 * dtile
                            S.mm(psum[:, pb, 0:n], [(wgate[:, k, col:col + 128], hTb[b][:, k, 0:n]) for k in range(8)],
                                 kg + ['hTb%d' % b], [pk[0]])
                            S.act(sgt[nb][:, 0:n], psum[:, pb, 0:n], AF.Sigmoid, [pk[0], 'bg'], ['sgt%d' % nb],
                                  bias=bg[:, 8 * nb + dtile:8 * nb + dtile + 1])
                            S.mm(psum[:, pb + 1, 0:n],
                                 [(wbr[:, 6 * nb + k, 128 * dtile:128 * dtile + 128], Yb[0][:, 6 * nb + k, 0:n]) for k in range(6)],
                                 kb + ['Yb0'], [pk[1]])
                            if nb == 0:
                                S.tt(macc[:, 0:n], sgt[nb][:, 0:n], psum[:, pb + 1, 0:n], ALU.mult, ['sgt%d' % nb, pk[1]], ['macc'])
                            else:
                                S.tt(sgt[nb][:, 0:n], sgt[nb][:, 0:n], psum[:, pb + 1, 0:n], ALU.mult, ['sgt%d' % nb, pk[1]], ['sgt%d' % nb])
                                if nb == 1:
                                    S.tt(macc[:, 0:n], macc[:, 0:n], sgt[nb][:, 0:n], ALU.add, ['macc', 'sgt%d' % nb], ['macc'], eng='pool')
                                else:
                                    S.tt(mg[:, dtile, 0:n], macc[:, 0:n], sgt[nb][:, 0:n], ALU.add, ['macc', 'sgt%d' % nb], [('mg', dtile)],
                                         eng='pool')
                    mgk = [('mg', d_) for d_ in range(8)]
                    for ti in range(n // 128):
                        i = t0 // 128 + ti
                        hb = tcount % 2
                        tcount += 1
                        S.dma(hin[hb][:, :], h_dram[128 * i:128 * i + 128, :], [('h', i)], ['hin%d' % hb])
                        pb, pk = psalloc(2)
                        for half in range(2):
                            S.mm(psum[:, pb + half, :],
                                 [(mg[:, k, 128 * ti:128 * ti + 128], wo[:, k, 512 * half:512 * half + 512]) for k in range(8)],
                                 mgk + ko, [pk[half]])
                            S.stt(hin[hb][:, 512 * half:512 * half + 512], hin[hb][:, 512 * half:512 * half + 512], float(ALPHA),
                                  psum[:, pb + half, :], ALU.mult, ALU.add, ['hin%d' % hb, pk[half]], ['hin%d' % hb])
                        layernorm(lnt, hin[hb][:, :], hout[hb][:, :], gb[:, :], bb[:, :], ['hin%d' % hb], ['hout%d' % hb])
                        if last_layer:
                            lo = 128 * i - 16
                            if i == 0:
                                S.dma(y_out[0:112, :], hout[hb][16:128, :], ['hout%d' % hb], [('yo', i)], q='act')
                            elif i < 32:
                                S.dma(y_out[lo:lo + 128, :], hout[hb][:, :], ['hout%d' % hb], [('yo', i)], q='act')
                            else:
                                S.dma(y_out[lo:lo + 16, :], hout[hb][0:16, :], ['hout%d' % hb], [('yo', i)], q='act')
                        else:
                            S.dma(h_dram[128 * i:128 * i + 128, :], hout[hb][:, :], ['hout%d' % hb], [('h', i)], q='act')
                            pt, pkt = psalloc(2)
                            for k in range(8):
                                S.tr(psum[:, pt + k // 4, (k % 4) * 128:(k % 4) * 128 + 128], hout[hb][:, 128 * k:128 * k + 128], ident,
                                     ['hout%d' % hb, 'cst'], [pkt[k // 4]])
                            S.copy('act', htb[hb][:, 0:4, :], psum[:, pt, :].rearrange("p (a c) -> p a c", a=4), [pkt[0]], ['htb%d' % hb])
                            S.copy('dve', htb[hb][:, 4:8, :], psum[:, pt + 1, :].rearrange("p (a c) -> p a c", a=4), [pkt[1]],
                                   ['htb%db' % hb])
                            S.dma(hT_dram[:, :, 128 * i:128 * i + 128], htb[hb][:, :, :], ['htb%d' % hb, 'htb%db' % hb], [('hT', i)], q='act')
            S.barrier()
            if debug and l == nlayers - 1:
                S.dma(dbg['act_tm'][:, :], act_tm[:, :], [], ['dbg1'])
                S.dma(dbg['ptm'][:, :], ptm[:, :], [], ['dbg2'])
                S.dma(dbg['Y'][:, :, :], Y_dram[:, :, :], [], ['dbg3'])
                S.dma(dbg['h'][:, :], h_dram[:, :], [], ['dbg4'])
                S.barrier()
        S.finish()
    return nc


PARAM_NAMES = ['meta', 'ln_in_g', 'ln_in_b', 'w_in', 's5_a_re', 's5_a_im', 's5_log_step', 's5_b_re', 's5_b_im',
               's5_c_re', 's5_c_im', 's5_d', 's5_w_glu', 's5_b_glu', 'ssd_conv_w', 'ssd_conv_b', 'ssd_dt_bias',
               'ssd_a_log', 'ssd_d', 'ssd_norm_g', 'gdn_conv_w', 'gdn_dt_bias', 'gdn_a_log', 'gdn_norm_g',
               'w_branch', 'b_gate', 'w_out', 'ln_g', 'ln_b']


def kernel(**inputs):
    x = np.ascontiguousarray(np.asarray(inputs['x'], dtype=np.float32))
    nb = x.shape[0]
    params = {k: np.ascontiguousarray(np.asarray(inputs[k], dtype=np.float32)) for k in PARAM_NAMES}
    consts = make_consts()
    nc = build()
    in_maps = []
    for b in range(nb):
        m = dict(params)
        m['x'] = x[b]
        m['consts'] = consts
        in_maps.append(m)
    res = run_bass_kernel_spmd(nc, in_maps, core_ids=list(range(nb)))
    out = np.stack([np.asarray(res.results[b]['y'], dtype=np.float32) for b in range(nb)], axis=0)
    return out
```
